# Optimizing a Trainium2 kernel written in Bass

```python
import jax, jax.numpy as jnp
from jax import lax
import numpy as np

D_MODEL = 1024
BATCH = 32
SEQ = 256
DEPTH = 2
DEC_BATCH = 4
DEC_SEQ = 1024
PAST_LEN = 256

GRID_W = 64
HEAD_DIM = 64
N_RWKV_HEADS = 8
RWKV_WIDTH = N_RWKV_HEADS * HEAD_DIM
N_Q_HEADS = 8
N_KV_HEADS = 2
GQA_GROUP = N_Q_HEADS // N_KV_HEADS
ATTN_WIDTH = N_Q_HEADS * HEAD_DIM
KV_WIDTH = N_KV_HEADS * HEAD_DIM
DECAY_RANK = 64
ICLR_RANK = 64
GATE_RANK = 128
D_FF = 4 * D_MODEL
Q_BLOCK = 128
ROPE_THETA = 10000.0
ROPE_FREQS = HEAD_DIM // 4
N_DIRS = 2
N_MOD = 6
NORM_EPS = 1e-6
GN_EPS = 64e-5
DECAY_SCALE = 0.606531
SPLIT_SIZES = (RWKV_WIDTH, RWKV_WIDTH, RWKV_WIDTH, DECAY_RANK, ICLR_RANK, GATE_RANK,
               ATTN_WIDTH, KV_WIDTH, KV_WIDTH, 2 * D_MODEL)
D_IN = 3 * RWKV_WIDTH + DECAY_RANK + ICLR_RANK + GATE_RANK + ATTN_WIDTH + 2 * KV_WIDTH + 2 * D_MODEL

kernel_name = "hybrid_rwkv7_gqa_dit_step"

F32 = jnp.float32


def rms_norm(x, g):
    xf = x.astype(F32)
    y = xf * lax.rsqrt(jnp.mean(xf * xf, axis=-1, keepdims=True) + NORM_EPS)
    return (y * g.astype(F32)).astype(x.dtype)


def centred_shift(z):
    zp = jnp.pad(z, ((0, 0), (1, 1), (0, 0)))
    return 0.5 * (zp[:, :-2] + zp[:, 2:])


def axial_rope_tables(n_rows):
    rows = jnp.repeat(jnp.arange(n_rows, dtype=F32), GRID_W)
    cols = jnp.tile(jnp.arange(GRID_W, dtype=F32), n_rows)
    half = HEAD_DIM // 2
    freqs = 1.0 / (ROPE_THETA ** (jnp.arange(0, half, 2, dtype=F32) / half))
    ang = jnp.stack([rows[:, None] * freqs, cols[:, None] * freqs], axis=1)
    return jnp.cos(ang), jnp.sin(ang)


def apply_rope(x, cos, sin):
    B, T, H, _ = x.shape
    xr = x.astype(F32).reshape(B, T, H, 2, 2, ROPE_FREQS)
    x1, x2 = xr[..., 0, :], xr[..., 1, :]
    c = cos[None, :, None]
    s = sin[None, :, None]
    out = jnp.stack([x1 * c - x2 * s, x2 * c + x1 * s], axis=-2)
    return out.reshape(B, T, H, HEAD_DIM).astype(x.dtype)


def block_attention(q, k, v):
    B, T = q.shape[0], q.shape[1]
    nb = T // Q_BLOCK
    qb = q.reshape(B, nb, Q_BLOCK, N_KV_HEADS, GQA_GROUP, HEAD_DIM).transpose(1, 0, 2, 3, 4, 5)
    scale = HEAD_DIM ** -0.5

    def one_block(q_blk):
        s = jnp.einsum('bqhgd,bkhd->bhgqk', q_blk, k).astype(F32) * scale
        p = jax.nn.softmax(s, axis=-1)
        return jnp.einsum('bhgqk,bkhd->bqhgd', p.astype(v.dtype), v)

    o = lax.map(one_block, qb)
    return o.transpose(1, 0, 2, 3, 4, 5).reshape(B, T, ATTN_WIDTH)


def wkv_scan(S0, r, w, kt, v, kh, a):
    xs = tuple(jnp.moveaxis(z.astype(F32), 2, 0) for z in (r, w, kt, v, kh, a))

    def step(S, inp):
        r_t, w_t, kt_t, v_t, kh_t, a_t = inp
        S_kh = jnp.einsum('bdhvk,bdhk->bdhv', S, kh_t)
        S = S * w_t[..., None, :] - S_kh[..., :, None] * (a_t * kh_t)[..., None, :] \
            + v_t[..., :, None] * kt_t[..., None, :]
        return S, jnp.einsum('bdhvk,bdhk->bdhv', S, r_t)

    S_fin, ys = lax.scan(step, S0.astype(F32), xs)
    return S_fin, jnp.moveaxis(ys, 0, 2)


def rwkv_branch(r_p, k_p, v_p, lw, la, lg, S0, P, l):
    B, T, C = r_p.shape
    dt = r_p.dtype
    mu = P['rwkv_mu'][l]
    r = r_p + mu[0] * (centred_shift(r_p) - r_p)
    k = k_p + mu[1] * (centred_shift(k_p) - k_p)
    v = v_p + mu[2] * (centred_shift(v_p) - v_p)
    heads = lambda z: z.reshape(z.shape[:-1] + (N_RWKV_HEADS, HEAD_DIM))
    kkf = heads(k * P['rwkv_k_k'][l]).astype(F32)
    kh = kkf * lax.rsqrt(jnp.sum(kkf * kkf, axis=-1, keepdims=True) + 1e-12)
    w_raw = jnp.einsum('btr,drc->bdtc', jnp.tanh(lw), P['decay_up'][l]) + P['decay_w0'][l][None, :, None, :]
    w = jnp.exp(-DECAY_SCALE * jax.nn.sigmoid(w_raw.astype(F32)))
    a = jax.nn.sigmoid((jnp.einsum('btr,drc->bdtc', la, P['iclr_up'][l])
                        + P['iclr_a0'][l][None, :, None, :]).astype(F32))
    kt = k[:, None].astype(F32) * (1.0 + (a - 1.0) * P['rwkv_k_a'][l].astype(F32))
    both = lambda z: jnp.stack([z, jnp.flip(z, axis=1)], axis=1)
    orient = lambda z: jnp.stack([z[:, 0], jnp.flip(z[:, 1], axis=1)], axis=1)
    rh, vh = heads(r), heads(v)
    kt_h = heads(kt)
    S_fin, y = wkv_scan(S0, both(rh), orient(heads(w)), orient(kt_h), both(vh), both(kh), orient(heads(a)))
    y = y[:, 0] + jnp.flip(y[:, 1], axis=1)
    mean = jnp.mean(y, axis=-1, keepdims=True)
    var = jnp.mean(jnp.square(y - mean), axis=-1, keepdims=True)
    yn = (y - mean) * lax.rsqrt(var + GN_EPS) * P['gn_w'][l].reshape(N_RWKV_HEADS, HEAD_DIM) \
        + P['gn_b'][l].reshape(N_RWKV_HEADS, HEAD_DIM)
    bonus = jnp.einsum('bthn,bdthn,hn->bth', rh.astype(F32), kt_h, P['rwkv_r_k'][l].astype(F32))[..., None] \
        * vh.astype(F32)
    g = jax.nn.sigmoid(lg) @ P['gate_up'][l]
    out = (yn + bonus).reshape(B, T, C).astype(dt) * g
    return out, S_fin.astype(dt)


def mixer(h, l, P, rope, ctx_k, ctx_v, S0):
    B, T, _ = h.shape
    proj = h @ P['w_in'][l]
    idx = np.cumsum(SPLIT_SIZES)[:-1].tolist()
    r_p, k_p, v_p, lw, la, lg, q_p, ka_p, va_p, gates = jnp.split(proj, idx, axis=-1)
    o_r, S_fin = rwkv_branch(r_p, k_p, v_p, lw, la, lg, S0, P, l)
    q = rms_norm(q_p.reshape(B, T, N_Q_HEADS, HEAD_DIM), P['q_gain'][l])
    k = rms_norm(ka_p.reshape(B, T, N_KV_HEADS, HEAD_DIM), P['k_gain'][l])
    v = va_p.reshape(B, T, N_KV_HEADS, HEAD_DIM)
    if rope is not None:
        q = apply_rope(q, rope[0], rope[1])
        k = apply_rope(k, rope[0], rope[1])
    k_own, v_own = k, v
    if ctx_k is not None:
        k = jnp.concatenate([ctx_k, k], axis=1)
        v = jnp.concatenate([ctx_v, v], axis=1)
    o_a = block_attention(q, k, v)
    g_r, g_a = jnp.split(gates, 2, axis=-1)
    merged = jax.nn.sigmoid(g_r) * (o_r @ P['w_br'][l, 0]) + jax.nn.sigmoid(g_a) * (o_a @ P['w_br'][l, 1])
    return merged @ P['w_out'][l], k_own, v_own, S_fin


def trunk_layer(x, cvec, l, P, rope, ctx_k, ctx_v, S0):
    mod = (jax.nn.silu(cvec) @ P['w_mod'][l] + P['b_mod'][l]).reshape(-1, 1, N_MOD * D_MODEL)
    sh1, sc1, g1, sh2, sc2, g2 = jnp.split(mod, N_MOD, axis=-1)
    ng = P['norm_g'][l]
    h = rms_norm(x, ng[0]) * (1.0 + sc1) + sh1
    m, k_new, v_new, S_fin = mixer(h, l, P, rope, ctx_k, ctx_v, S0)
    x = x + g1 * rms_norm(m, ng[1])
    h = rms_norm(x, ng[2]) * (1.0 + sc2) + sh2
    f = jnp.square(jax.nn.relu(h @ P['mlp_up'][l])) @ P['mlp_down'][l]
    x = x + g2 * rms_norm(f, ng[3])
    return x, k_new, v_new, S_fin


def setup_inputs(seed: int = 0) -> dict:
    key = jax.random.key(seed)
    ks = jax.random.split(key, 32)
    nrm = lambda k, shape, s: jax.random.normal(k, shape, F32) * s
    H, N, C = N_RWKV_HEADS, HEAD_DIM, RWKV_WIDTH
    return {
        'x_prompt': nrm(ks[0], (BATCH, SEQ, D_MODEL), 1.0),
        'x_sample': nrm(ks[1], (DEC_BATCH, DEC_SEQ, D_MODEL), 1.0),
        'cache_k': nrm(ks[2], (DEC_BATCH, DEPTH, PAST_LEN, N_KV_HEADS, HEAD_DIM), 1.0),
        'cache_v': nrm(ks[3], (DEC_BATCH, DEPTH, PAST_LEN, N_KV_HEADS, HEAD_DIM), 1.0),
        'state_wkv': nrm(ks[4], (DEC_BATCH, DEPTH, N_DIRS, H, N, N), 0.1),
        'c': nrm(ks[5], (DEC_BATCH, D_MODEL), 1.0),
        'c_ctx': nrm(ks[6], (D_MODEL,), 1.0),
        'w_in': nrm(ks[7], (DEPTH, D_MODEL, D_IN), D_MODEL ** -0.5),
        'w_br': nrm(ks[8], (DEPTH, 2, C, D_MODEL), C ** -0.5),
        'w_out': nrm(ks[9], (DEPTH, D_MODEL, D_MODEL), D_MODEL ** -0.5),
        'w_mod': nrm(ks[10], (DEPTH, D_MODEL, N_MOD * D_MODEL), D_MODEL ** -0.5),
        'b_mod': nrm(ks[11], (DEPTH, N_MOD * D_MODEL), 0.02),
        'norm_g': 1.0 + nrm(ks[12], (DEPTH, 4, D_MODEL), 0.05),
        'mlp_up': nrm(ks[13], (DEPTH, D_MODEL, D_FF), D_MODEL ** -0.5),
        'mlp_down': nrm(ks[14], (DEPTH, D_FF, D_MODEL), D_FF ** -0.5),
        'rwkv_mu': 0.5 + nrm(ks[15], (DEPTH, 3, C), 0.1),
        'rwkv_k_k': 0.85 + nrm(ks[16], (DEPTH, C), 0.05),
        'rwkv_k_a': 1.0 + nrm(ks[17], (DEPTH, C), 0.05),
        'rwkv_r_k': nrm(ks[18], (DEPTH, H, N), 0.1),
        'decay_w0': -1.0 + nrm(ks[19], (DEPTH, N_DIRS, C), 0.5),
        'decay_up': nrm(ks[20], (DEPTH, N_DIRS, DECAY_RANK, C), 0.1),
        'iclr_a0': nrm(ks[21], (DEPTH, N_DIRS, C), 0.3),
        'iclr_up': nrm(ks[22], (DEPTH, N_DIRS, ICLR_RANK, C), 0.1),
        'gate_up': nrm(ks[23], (DEPTH, GATE_RANK, C), GATE_RANK ** -0.5),
        'gn_w': 1.0 + nrm(ks[24], (DEPTH, C), 0.05),
        'gn_b': nrm(ks[25], (DEPTH, C), 0.02),
        'q_gain': 1.0 + nrm(ks[26], (DEPTH, HEAD_DIM), 0.05),
        'k_gain': 1.0 + nrm(ks[27], (DEPTH, HEAD_DIM), 0.05),
    }


def reference(x_prompt, x_sample, cache_k, cache_v, state_wkv, c, c_ctx, w_in, w_br, w_out, w_mod, b_mod,
              norm_g, mlp_up, mlp_down, rwkv_mu, rwkv_k_k, rwkv_k_a, rwkv_r_k, decay_w0, decay_up, iclr_a0,
              iclr_up, gate_up, gn_w, gn_b, q_gain, k_gain):
    P = dict(w_in=w_in, w_br=w_br, w_out=w_out, w_mod=w_mod, b_mod=b_mod, norm_g=norm_g, mlp_up=mlp_up,
             mlp_down=mlp_down, rwkv_mu=rwkv_mu, rwkv_k_k=rwkv_k_k, rwkv_k_a=rwkv_k_a, rwkv_r_k=rwkv_r_k,
             decay_w0=decay_w0, decay_up=decay_up, iclr_a0=iclr_a0, iclr_up=iclr_up, gate_up=gate_up,
             gn_w=gn_w, gn_b=gn_b, q_gain=q_gain, k_gain=k_gain)

    b_ctx = x_prompt.shape[0]
    S_zero = jnp.zeros((b_ctx, N_DIRS, N_RWKV_HEADS, HEAD_DIM, HEAD_DIM), x_prompt.dtype)
    y_prompt = x_prompt
    ks_list, vs_list, ss_list = [], [], []
    for l in range(DEPTH):
        y_prompt, k_l, v_l, s_l = trunk_layer(y_prompt, c_ctx, l, P, None, None, None, S_zero)
        ks_list.append(k_l)
        vs_list.append(v_l)
        ss_list.append(s_l)
    new_cache_k = jnp.stack(ks_list, axis=1)
    new_cache_v = jnp.stack(vs_list, axis=1)
    new_state_wkv = jnp.stack(ss_list, axis=1)

    n_rows = x_sample.shape[1] // GRID_W
    rope = axial_rope_tables(n_rows)
    y_sample = x_sample
    for l in range(DEPTH):
        y_sample, _, _, _ = trunk_layer(y_sample, c, l, P, rope, cache_k[:, l], cache_v[:, l], state_wkv[:, l])

    return (y_prompt, y_sample, new_cache_k, new_cache_v, new_state_wkv)
```

```python
import numpy as np
from contextlib import ExitStack
import concourse.bass as bass
import concourse.mybir as mybir
from concourse.bass_utils import run_bass_kernel_spmd

F32 = mybir.dt.float32
BF16 = mybir.dt.bfloat16
AF = mybir.ActivationFunctionType
ALU = mybir.AluOpType

NL = 2
DM = 1024
NTOK = 1024
DIN = 4608
LAM = 0.606531
NORM_EPS = 1e-6
GN_EPS = 64e-5
NV = 384
SLOT = 4096

def i_ng(l, k, c): return l * 32 + k * 8 + c
def i_bmod(l, j): return 64 + l * 48 + j
def i_mu(l, i, c): return 160 + l * 12 + i * 4 + c
def i_kk(l, c): return 184 + l * 4 + c
def i_ka(l, c): return 192 + l * 4 + c
def i_rk(l, c): return 200 + l * 4 + c
def i_w0(l, d, c): return 208 + l * 8 + d * 4 + c
def i_a0(l, d, c): return 224 + l * 8 + d * 4 + c
def i_gnw(l, c): return 240 + l * 4 + c
def i_gnb(l, c): return 248 + l * 4 + c
def i_qg(l): return 256 + l
def i_kg(l): return 258 + l
I_CCTX = 260
I_CS = 268


class T:
    __slots__ = ("w", "r", "dsem", "dcnt")

    def __init__(self):
        self.w = None
        self.r = {}
        self.dsem = None
        self.dcnt = 0


class Sched:
    EPOCH = 4000

    def __init__(self, nc, es):
        self.nc = nc
        self.es = es
        self.eng = {"pe": nc.tensor, "dve": nc.vector, "act": nc.scalar, "pool": nc.gpsimd, "sp": nc.sync}
        self.cnt = {k: 0 for k in self.eng}
        self.sems = {k: [] for k in self.eng}
        self.seen = {k: {} for k in self.eng}
        self.nsem = 0
        self.nwait = 0

    def new_sem(self, name):
        self.nsem += 1
        return self.es.enter_context(self.nc.semaphore(f"{name}_{self.nsem}"))

    def _sem_for(self, e, idx):
        ep = (idx - 1) // self.EPOCH
        while len(self.sems[e]) <= ep:
            self.sems[e].append(self.new_sem(f"s_{e}"))
        return self.sems[e][ep], (idx - 1) % self.EPOCH + 1

    def _wait(self, on, dep):
        if dep is None:
            return
        if dep[0] == "dma":
            _, sem, val = dep
            key = ("dma", id(sem))
            if self.seen[on].get(key, 0) >= val:
                return
            self.eng[on].wait_ge(sem, val)
            self.nwait += 1
            self.seen[on][key] = val
        else:
            e, idx = dep
            if e == on and on == "pe":
                return
            if self.seen[on].get(e, 0) >= idx:
                return
            sem, val = self._sem_for(e, idx)
            self.eng[on].wait_ge(sem, val)
            self.nwait += 1
            self.seen[on][e] = idx

    def deps(self, on, reads, writes):
        for t in reads:
            self._wait(on, t.w)
        for t in writes:
            self._wait(on, t.w)
            for k, v in t.r.items():
                if isinstance(k, tuple):
                    self._wait(on, ("dma", k[1], v))
                else:
                    self._wait(on, (k, v))

    def op(self, on, reads, writes, fn):
        self.deps(on, reads, writes)
        ins = fn(self.eng[on])
        self.cnt[on] += 1
        n = self.cnt[on]
        sem, val = self._sem_for(on, n)
        ins.then_inc(sem, 1)
        for t in reads:
            t.r[on] = n
        for t in writes:
            t.w = (on, n)
            t.r = {}
        return ins

    def dma(self, on, out_ap, in_ap, reads, writes, **kw):
        self.deps(on, reads, writes)
        tgt = writes[0] if writes else reads[0]
        if tgt.dsem is None:
            tgt.dsem = self.new_sem("d")
        tgt.dcnt += 16
        ins = self.eng[on].dma_start(out=out_ap, in_=in_ap, **kw)
        ins.then_inc(tgt.dsem, 16)
        dep = ("dma", tgt.dsem, tgt.dcnt)
        for t in writes:
            t.w = dep
            t.r = {}
        for t in reads:
            t.r[("dma", tgt.dsem)] = tgt.dcnt
        return dep

    def barrier(self):
        es_ = ("pe", "dve", "act")
        for a in ("pe", "dve", "act", "pool", "sp"):
            for b in es_:
                if a != b and self.cnt[b]:
                    self._wait(a, (b, self.cnt[b]))

    def finish(self, out_deps):
        best = {}
        for d in out_deps:
            k = id(d[1])
            if k not in best or best[k][2] < d[2]:
                best[k] = d
        for d in best.values():
            self._wait("sp", d)
        for e in ("pe", "dve", "act"):
            if self.cnt[e]:
                self._wait("sp", (e, self.cnt[e]))


class Buf:
    _n = [0]

    def __init__(self, nc, es, name, shape, dt):
        Buf._n[0] += 1
        self.a = es.enter_context(nc.sbuf_tensor(f"{name}_{Buf._n[0]}", shape, dt))
        self.ts = {}

    def t(self, *key):
        if key not in self.ts:
            self.ts[key] = T()
        return self.ts[key]


def build(dbg=False, stop=None, units=(0, 1), layers=(0, 1)):
    nc = bass.Bass("TRN2", target_bir_lowering=False)
    dI = lambda n, s: nc.dram_tensor(n, s, F32, kind="ExternalInput").ap()
    dO = lambda n, s: nc.dram_tensor(n, s, F32, kind="ExternalOutput").ap()
    xin = [dI("xp", [NTOK, DM]), dI("xs", [NTOK, DM])]
    ck_d = dI("ck", [NL, 256, 128])
    cv_d = dI("cv", [NL, 256, 128])
    st_d = dI("st", [NL, 2, 8, 64, 64])
    vt_d = dI("vt", [NV, 128])
    cstf_d = dI("cstf", [128, 768])
    cstb_d = dI("cstb", [128, 1728 + 7 * 512 + 256])
    rope_d = dI("rope", [128, 2048])
    w_in = dI("w_in", [NL, DM, DIN]).rearrange("l (kc p) n -> l p kc n", p=128)
    w_br = dI("w_br", [NL, 2, 512, DM]).rearrange("l i (kc p) n -> l p (i kc) n", p=128)
    w_out = dI("w_out", [NL, DM, DM]).rearrange("l (kc p) n -> l p kc n", p=128)
    w_mod = dI("w_mod", [NL, DM, 6 * DM]).rearrange("l (kc p) n -> l p kc n", p=128)
    mlp_up = dI("mlp_up", [NL, DM, 4 * DM]).rearrange("l (kc p) n -> l p kc n", p=128)
    mlp_dn = dI("mlp_down", [NL, 4 * DM, DM]).rearrange("l (kc p) n -> l p kc n", p=128)
    dec_up = dI("decay_up", [NL, 2, 64, 512])
    icl_up = dI("iclr_up", [NL, 2, 64, 512])
    gate_up = dI("gate_up", [NL, 128, 512])
    yout = [dO("yp", [NTOK, DM]), dO("ys", [NTOK, DM])]
    nck_d = dO("nck", [4, NL, 256, 128])
    ncv_d = dO("ncv", [4, NL, 256, 128])
    nst_d = dO("nst", [4, NL, 2, 8, 64, 64])
    dumps = {}
    out_deps = []

    with ExitStack() as es:
        S = Sched(nc, es)
        mk = lambda name, shape, dt, st=es: Buf(nc, st, name, shape, dt)

        X = mk("X", [128, 8, NTOK], F32)
        H = mk("H", [128, 8, NTOK], BF16)
        SLOTS = [mk(f"slot{i}", [128, SLOT], BF16) for i in range(3)]
        CSTF = mk("CSTF", [128, 768], F32)
        CSTB = mk("CSTB", [128, 1728 + 7 * 512 + 256], BF16)
        VTT = mk("VTT", [128, NV], F32)
        DER = mk("DER", [128, 256], F32)
        MOD = mk("MOD", [128, NL, 2, 48], F32)
        DU = mk("DU", [128, NL, 2, 512], BF16)
        GU = mk("GU", [128, NL, 512], BF16)
        STG = [mk(f"STG{i}", [128, 1024], F32) for i in range(2)]
        SO = mk("SO", [64, 2, 128], F32)
        PSB = []
        for i in range(8):
            p = es.enter_context(nc.psum_tensor(f"ps{i}", [128, 512], F32))
            PSB.append((p, T()))

        IDENT = CSTF.a[:, 0:128]
        BLK = CSTF.a[:, 128:256]
        ONES = CSTF.a[:, 256:384]
        ROT = CSTF.a[:, 384:512]
        SEL = [CSTF.a[:, 512:640], CSTF.a[:, 640:768]]
        tCF = CSTF.t()
        MASKA = CSTB.a[:, 0:512]
        MASKB = CSTB.a[:, 512:1024]
        MASKC = CSTB.a[:, 1024:1536]
        IDB = CSTB.a[:, 1536:1664]
        ONESB = CSTB.a[:, 1664:1728]
        LMASK = [CSTB.a[:, 1728 + i * 512:1728 + (i + 1) * 512] for i in range(7)]
        ONES16 = CSTB.a[:, 5312:5440]
        BLK16 = CSTB.a[:, 5440:5568]
        tCB = CSTB.t()
        tV = VTT.t()
        tD = DER.t()
        tM = MOD.t()
        vcol = lambda i, n=1: VTT.a[:, i:i + n]
        D_OMM, D_HMU, D_OMKA, D_A1, D_G1, D_A2, D_G2 = 0, 24, 48, 56, 88, 120, 152
        dcol = lambda i, n=1: DER.a[:, i:i + n]
        d_lu = lambda base, l, u, c: base + (l * 2 + u) * 8 + c

        def MM(out, lhsT, rhs, R, W, start=True, stop=True):
            S.op("pe", R, W, lambda e: e.matmul(out, lhsT=lhsT, rhs=rhs, start=start, stop=stop))

        def ACT(out, in_, func, R, W, scale=1.0, bias=None):
            if bias is None:
                S.op("act", R, W, lambda e: e.activation(out=out, in_=in_, func=func, scale=scale))
            else:
                S.op("act", R, W, lambda e: e.activation(out=out, in_=in_, func=func, scale=scale, bias=bias))

        def TT(out, in0, in1, op, R, W, eng="dve"):
            S.op(eng, R, W, lambda e: e.tensor_tensor(out=out, in0=in0, in1=in1, op=op))

        def TS(out, in0, s1, s2, op0, op1, R, W, eng="dve"):
            if op1 is None:
                S.op(eng, R, W, lambda e: e.tensor_scalar(out=out, in0=in0, scalar1=s1, scalar2=None, op0=op0))
            else:
                S.op(eng, R, W, lambda e: e.tensor_scalar(out=out, in0=in0, scalar1=s1, scalar2=s2, op0=op0, op1=op1))

        def STT(out, in0, sc, in1, op0, op1, R, W, eng="dve"):
            S.op(eng, R, W, lambda e: e.scalar_tensor_tensor(out=out, in0=in0, scalar=sc, in1=in1, op0=op0, op1=op1))

        def CP(out, in_, R, W, eng="dve"):
            if eng == "act":
                ACT(out, in_, AF.Copy, R, W)
            else:
                S.op(eng, R, W, lambda e: e.tensor_copy(out=out, in_=in_))

        def RSQ(out, in_, scale, bias, R, W):
            S.op("act", R, W, lambda e: e.activation(out=out, in_=in_, func=AF.Ln, scale=scale, bias=bias))
            S.op("act", W, W, lambda e: e.activation(out=out, in_=out, func=AF.Exp, scale=-0.5))

        evc = [0]

        def EV(out, in_, R, W):
            evc[0] += 1
            CP(out, in_, R, W, eng=("act" if evc[0] % 2 else "dve"))

        psc = [0]

        def ps_rot(lo=0, hi=6):
            psc[0] += 1
            return PSB[lo + psc[0] % (hi - lo)]

        dcount = [0]

        def dump(name, ap, t, shape, dt=F32):
            if not dbg:
                return
            d = nc.dram_tensor("dbg_" + name, shape, dt, kind="ExternalOutput").ap()
            out_deps.append(S.dma("sp", d, ap, t if isinstance(t, list) else [t], []))
            dumps[name] = (shape, dt)

        class WS:
            def __init__(self):
                self.plan = []
                self.nxt = 0
                self.iss = 0
                self.busy = [False] * len(SLOTS)

            def add(self, tag, parts):
                self.plan.append((tag, parts))

            def _pump(self):
                while self.iss < len(self.plan) and not self.busy[self.iss % len(SLOTS)]:
                    tag, parts = self.plan[self.iss]
                    sl = SLOTS[self.iss % len(SLOTS)]
                    for (off, a, b, src) in parts:
                        dst = sl.a[:, off:off + a * b].rearrange("p (a b) -> p a b", b=b)
                        S.dma("pool", dst, src, [], [sl.t()])
                    self.busy[self.iss % len(SLOTS)] = True
                    self.iss += 1

            def pop(self, tag):
                i = self.nxt
                assert self.plan[i][0] == tag, (self.plan[i][0], tag)
                self._pump()
                assert self.iss > i, ("weight slot deadlock", tag)
                self.nxt += 1
                sl = SLOTS[i % len(SLOTS)]
                return sl.a, sl.t(), i

            def rel(self, i):
                self.busy[i % len(SLOTS)] = False
                self._pump()

        ws = WS()
        for u in units:
            for l in layers:
                ws.add(f"lw{u}{l}", [(0, 8, 256, w_in[l][:, :, 1536:1792])])
                for c in range(4):
                    ws.add(f"rkv{u}{l}{c}", [(i * 1024, 8, 128, w_in[l][:, :, i * 512 + c * 128:i * 512 + (c + 1) * 128]) for i in range(3)])
                ws.add(f"q{u}{l}", [(0, 8, 512, w_in[l][:, :, 1792:2304])])
                ws.add(f"kv{u}{l}", [(0, 8, 256, w_in[l][:, :, 2304:2560])])
                for jj in range(2):
                    ws.add(f"g{u}{l}{jj}", [(0, 8, 512, w_in[l][:, :, 2560 + jj * 512:2560 + (jj + 1) * 512])])
                    ws.add(f"g{u}{l}{2 + jj}", [(0, 8, 512, w_in[l][:, :, 3584 + jj * 512:3584 + (jj + 1) * 512])])
                for jt in range(2):
                    ws.add(f"wo{u}{l}{jt}", [(0, 8, 512, w_out[l][:, :, jt * 512:(jt + 1) * 512])])
                for tb in range(8):
                    ws.add(f"up{u}{l}{tb}", [(0, 8, 512, mlp_up[l][:, :, tb * 512:(tb + 1) * 512])])
                for j in range(8):
                    ws.add(f"dn{u}{l}{j}", [(0, 32, 128, mlp_dn[l][:, :, j * 128:(j + 1) * 128])])

        S.dma("sp", CSTF.a[:], cstf_d, [], [tCF])
        S.dma("pool", CSTB.a[:], cstb_d, [], [tCB])
        for l in range(NL):
            for d in range(2):
                S.dma("pool", DU.a[0:64, l, d, :], dec_up[l, d], [], [DU.t()])
                S.dma("pool", DU.a[64:128, l, d, :], icl_up[l, d], [], [DU.t()])
            S.dma("pool", GU.a[:, l, :], gate_up[l], [], [GU.t()])
        with ExitStack() as p0:
            VTS = mk("VTS", [128, 3, 128], F32, p0)
            WM = [mk(f"WM{i}", [128, 8, 512], BF16, p0) for i in range(3)]
            CSIL = mk("CSIL", [128, 16], BF16, p0)
            S.dma("sp", VTS.a[:], vt_d.rearrange("(a p) f -> p a f", p=128), [], [VTS.t()])
            pv, tpv = PSB[0]
            for a in range(3):
                MM(pv[:, a * 128:(a + 1) * 128], VTS.a[:, a, :], IDENT, [VTS.t(), tCF], [tpv])
            CP(VTT.a[:, 0:384], pv[:, 0:384], [tpv], [tV])
            ACT(CSIL.a[:], vcol(I_CCTX, 16), AF.Silu, [tV], [CSIL.t()])
            csv = CSIL.a[:].rearrange("p (u k) -> p u k", k=8)
            TS(dcol(D_OMM, 24), vcol(i_mu(0, 0, 0), 24), -1.0, 1.0, ALU.mult, ALU.add, [tV], [tD])
            TS(dcol(D_HMU, 24), vcol(i_mu(0, 0, 0), 24), 0.5, None, ALU.mult, None, [tV], [tD])
            TS(dcol(D_OMKA, 8), vcol(i_ka(0, 0), 8), -1.0, 1.0, ALU.mult, ALU.add, [tV], [tD])
            wmi = 0
            for l in range(NL):
                for blk in range(12):
                    wm = WM[wmi % 3]
                    wmi += 1
                    S.dma("pool", wm.a[:], w_mod[l][:, :, blk * 512:(blk + 1) * 512], [], [wm.t()])
                    pm, tpm = PSB[1 + blk % 2]
                    for jj in range(4):
                        for kc in range(8):
                            MM(pm[:, jj * 2:jj * 2 + 2], wm.a[:, kc, jj * 128:(jj + 1) * 128], csv[:, :, kc],
                               [wm.t(), CSIL.t()], [tpm], start=(kc == 0), stop=(kc == 7))
                    pmv = pm[:, 0:8].rearrange("p (j u) -> p j u", u=2)
                    for u in range(2):
                        TT(MOD.a[:, l, u, blk * 4:blk * 4 + 4], pmv[:, :, u], vcol(i_bmod(l, blk * 4), 4), ALU.add, [tpm, tV], [tM])
                for u in range(2):
                    STT(dcol(d_lu(D_A1, l, u, 0), 8), MOD.a[:, l, u, 8:16], 1.0, vcol(i_ng(l, 0, 0), 8), ALU.add, ALU.mult, [tM, tV], [tD])
                    TT(dcol(d_lu(D_G1, l, u, 0), 8), MOD.a[:, l, u, 16:24], vcol(i_ng(l, 1, 0), 8), ALU.mult, [tM, tV], [tD])
                    STT(dcol(d_lu(D_A2, l, u, 0), 8), MOD.a[:, l, u, 32:40], 1.0, vcol(i_ng(l, 2, 0), 8), ALU.add, ALU.mult, [tM, tV], [tD])
                    TT(dcol(d_lu(D_G2, l, u, 0), 8), MOD.a[:, l, u, 40:48], vcol(i_ng(l, 3, 0), 8), ALU.mult, [tM, tV], [tD])
            dump("mod", MOD.a[:].rearrange("p l u j -> p (l u j)"), tM, [128, NL * 2 * 48])
            S.barrier()
        if stop == "p0":
            S.finish(out_deps)
            return nc, dumps

        HS = [slice(0, 512), slice(512, 1024)]

        def rstd_from(src_fn, nch, lhsT, scale, eps, RSTD, tR, SQ):
            pss, tps = PSB[7]
            for c in range(nch):
                sq = SQ[c % 2]
                ap, ts_ = src_fn(c)
                ACT(sq.a[:], ap, AF.Square, ts_, [sq.t()])
                MM(pss[:], lhsT, sq.a[:], [sq.t(), tCB], [tps], start=(c == 0), stop=(c == nch - 1))
            RSQ(RSTD, pss[:], scale, eps, [tps], [tR])

        def norm_mod(u, l, A_base, sh_off, pst):
            SQ = [mk(f"nSQ{i}", [128, 512], BF16, pst) for i in range(2)]
            RS = mk("nRS", [128, 512], F32, pst)
            TMP = [mk(f"nTMP{i}", [128, 512], F32, pst) for i in range(2)]
            for h in range(2):
                rstd_from(lambda c: (X.a[:, c, HS[h]], [X.t(c, h)]), 8, ONES16, 1.0 / DM, NORM_EPS, RS.a[:], RS.t(), SQ)
                for c in range(8):
                    tmp = TMP[c % 2]
                    TT(tmp.a[:], X.a[:, c, HS[h]], RS.a[:], ALU.mult, [X.t(c, h), RS.t()], [tmp.t()])
                    ACT(H.a[:, c, HS[h]], tmp.a[:], AF.Identity, [tmp.t(), tD, tM], [H.t(c, h)],
                        scale=dcol(d_lu(A_base, l, u, c)), bias=MOD.a[:, l, u, sh_off + c:sh_off + c + 1])

        def resid_add(u, l, SRC, G_base, h, pst_bufs):
            SQ, RS, TMP = pst_bufs
            rstd_from(lambda c: (SRC.a[:, c, :], [SRC.t(c)]), 8, ONES16, 1.0 / DM, NORM_EPS, RS.a[:], RS.t(), SQ)
            for c in range(8):
                tmp = TMP[c % 2]
                TT(tmp.a[:], SRC.a[:, c, :], RS.a[:], ALU.mult, [SRC.t(c), RS.t()], [tmp.t()])
                STT(X.a[:, c, HS[h]], tmp.a[:], dcol(d_lu(G_base, l, u, c)), X.a[:, c, HS[h]], ALU.mult, ALU.add,
                    [tmp.t(), tD, X.t(c, h)], [X.t(c, h)])

        def load_x(u):
            for tt in range(8):
                stg = STG[tt % 2]
                S.dma("sp", stg.a[:], xin[u][tt * 128:(tt + 1) * 128, :], [], [stg.t()])
                for g in range(2):
                    pb, tpb = ps_rot()
                    for cc in range(4):
                        c = g * 4 + cc
                        MM(pb[:, cc * 128:(cc + 1) * 128], stg.a[:, c * 128:(c + 1) * 128], IDENT, [stg.t(), tCF], [tpb])
                    EV(X.a[:, g * 4:(g + 1) * 4, tt * 128:(tt + 1) * 128], pb[:].rearrange("p (c t) -> p c t", t=128),
                       [tpb], [X.t(c_, tt // 4) for c_ in range(g * 4, g * 4 + 4)])

        def store_y(u):
            for tt in range(8):
                stg = STG[tt % 2]
                for g in range(2):
                    pb, tpb = ps_rot()
                    for cc in range(4):
                        c = g * 4 + cc
                        MM(pb[:, cc * 128:(cc + 1) * 128], X.a[:, c, tt * 128:(tt + 1) * 128], IDENT, [X.t(c, tt // 4), tCF], [tpb])
                    EV(stg.a[:, g * 512:(g + 1) * 512], pb[:], [tpb], [stg.t()])
                out_deps.append(S.dma("sp", yout[u][tt * 128:(tt + 1) * 128, :], stg.a[:], [stg.t()], []))

        def layer(u, l):
            B = 4 if u == 0 else 1
            TL = NTOK // B
            nT = TL // 128
            with ExitStack() as pst:
                norm_mod(u, l, D_A1, 0, pst)
                S.barrier()
            if dbg and stop == "n1":
                dump("h", H.a[:].rearrange("p c t -> p (c t)"), list(H.ts.values()), [128, 8 * NTOK], BF16)
                return True
            with ExitStack() as mix:
                OR_ = mk("OR", [128, 4, NTOK], BF16, mix)
                LWLA = mk("LWLA", [128, NTOK], BF16, mix)
                SLG = mk("SLG", [128, NTOK], BF16, mix)
                wsl, twsl, wi = ws.pop(f"lw{u}{l}")
                wv = wsl[:, 0:2048].rearrange("p (a b) -> p a b", b=256)
                for h in range(2):
                    for ch in range(2):
                        pb, tpb = ps_rot()
                        for kc in range(8):
                            MM(pb[:], wv[:, kc, ch * 128:(ch + 1) * 128], H.a[:, kc, HS[h]], [twsl, H.t(kc, h)], [tpb], start=(kc == 0), stop=(kc == 7))
                        if ch == 0:
                            ACT(LWLA.a[0:64, HS[h]], pb[0:64, :], AF.Tanh, [tpb], [LWLA.t(h)])
                            CP(LWLA.a[64:128, HS[h]], pb[64:128, :], [tpb], [LWLA.t(h)])
                        else:
                            ACT(SLG.a[:, HS[h]], pb[:], AF.Sigmoid, [tpb], [SLG.t(h)])
                ws.rel(wi)
                if dbg and stop == "lw":
                    dump("lwla", LWLA.a[:], list(LWLA.ts.values()), [128, NTOK], BF16)
                    dump("slg", SLG.a[:], list(SLG.ts.values()), [128, NTOK], BF16)
                    return True
                with ExitStack() as rw:
                    r = rwkv(u, l, B, TL, nT, OR_, LWLA, SLG, rw)
                    S.barrier()
                if r:
                    return True
                OA_ = mk("OA", [128, 4, NTOK], BF16, mix)
                with ExitStack() as at:
                    r = attention(u, l, B, TL, OA_, at)
                    S.barrier()
                if r:
                    return True
                with ExitStack() as mg:
                    r = merge(u, l, OR_, OA_, mg)
                    S.barrier()
                if r:
                    return True
            with ExitStack() as pst:
                norm_mod(u, l, D_A2, 24, pst)
                S.barrier()
            with ExitStack() as ml:
                r = mlp(u, l, ml)
                S.barrier()
            return r

        def rwkv(u, l, B, TL, nT, OR_, LWLA, SLG, rw):
            f32b = lambda n, st: mk(n, [128, NTOK], F32, st)
            b16b = lambda n, st: mk(n, [128, NTOK], BF16, st)
            BON = f32b("BON", rw)
            GATE = b16b("GATE", rw)
            YACC = f32b("YACC", rw)
            RESET = f32b("RESET", rw)
            RT = [b16b(f"RTl{d}", rw) for d in range(2)]
            KTT = [b16b(f"KTT{d}", rw) for d in range(2)]
            BN = [b16b(f"BN{d}", rw) for d in range(2)]
            KHT = [b16b(f"KHT{d}", rw) for d in range(2)]
            VB = b16b("VB", rw)
            PC = mk("PC", [128, 2, 8], F32, rw)
            TOT = mk("TOT", [128, 8], F32, rw)
            G32 = mk("G32", [128, 4, 128], F32, rw)
            GB = mk("GB", [128, 4, 128], BF16, rw)
            GT = mk("GT", [128, 128], F32, rw)
            ST = mk("ST", [128, 2, 128], F32, rw)
            if u == 1:
                S.op("dve", [], [ST.t()], lambda e: e.memset(ST.a[:], 0.0))

            S.op("dve", [], [RESET.t()], lambda e: e.memset(RESET.a[:], 1.0))
            S.op("dve", [RESET.t()], [RESET.t()], lambda e: e.memset(RESET.a[:].rearrange("p (a b) -> p a b", b=128)[:, :, 0:1], 0.0))
            tile_ = lambda buf, tt: buf.a[:, tt * 128:(tt + 1) * 128]

            def prepA(c, pp):
                RKV = [f32b(f"RKV{i}", pp) for i in range(3)]
                TMP = [f32b(f"RT{i}", pp) for i in range(4)]
                KH = f32b("KH", pp)
                KTS = f32b("KTS", pp)
                wsl, twsl, wi = ws.pop(f"rkv{u}{l}{c}")
                wv = wsl[:, 0:3072].rearrange("p (i a b) -> p i a b", a=8, b=128)
                for i in range(3):
                    for h in range(2):
                        pb, tpb = ps_rot()
                        for kc in range(8):
                            MM(pb[:], wv[:, i, kc, :], H.a[:, kc, HS[h]], [twsl, H.t(kc, h)], [tpb], start=(kc == 0), stop=(kc == 7))
                        EV(RKV[i].a[:, HS[h]], pb[:], [tpb], [RKV[i].t(h)])
                ws.rel(wi)
                allh = lambda bf: [bf.t(0), bf.t(1)]
                for i in range(3):
                    Z = RKV[i]
                    zv = Z.a[:].rearrange("p (b t) -> p b t", t=TL)
                    tv = TMP[0].a[:].rearrange("p (b t) -> p b t", t=TL)
                    TT(tv[:, :, 1:TL - 1], zv[:, :, 0:TL - 2], zv[:, :, 2:TL], ALU.add, allh(Z), allh(TMP[0]))
                    CP(tv[:, :, 0:1], zv[:, :, 1:2], allh(Z), allh(TMP[0]))
                    CP(tv[:, :, TL - 1:TL], zv[:, :, TL - 2:TL - 1], allh(Z), allh(TMP[0]))
                    for h in range(2):
                        ACT(Z.a[:, HS[h]], Z.a[:, HS[h]], AF.Identity, [Z.t(h), tD, TMP[0].t(h)], [Z.t(h)], scale=dcol(D_OMM + l * 12 + i * 4 + c))
                    for h in range(2):
                        STT(Z.a[:, HS[h]], TMP[0].a[:, HS[h]], dcol(D_HMU + l * 12 + i * 4 + c), Z.a[:, HS[h]], ALU.mult, ALU.add,
                            [TMP[0].t(h), tD, Z.t(h)], [Z.t(h)])
                return RKV, TMP, KH, KTS

            def prepB(c, RKV, TMP, KH, KTS):
                allh = lambda bf: [bf.t(0), bf.t(1)]
                R_, K_, V_ = RKV
                H2 = range(2)
                for h in H2:
                    CP(VB.a[:, HS[h]], V_.a[:, HS[h]], [V_.t(h)], [VB.t(h)], eng="act")
                for h in H2:
                    ACT(TMP[0].a[:, HS[h]], K_.a[:, HS[h]], AF.Identity, [K_.t(h), tV], [TMP[0].t(h)], scale=vcol(i_kk(l, c)))
                for h in H2:
                    ACT(TMP[1].a[:, HS[h]], TMP[0].a[:, HS[h]], AF.Square, [TMP[0].t(h)], [TMP[1].t(h)])
                for h in H2:
                    pb, tpb = ps_rot()
                    MM(pb[:], BLK, TMP[1].a[:, HS[h]], [tCF, TMP[1].t(h)], [tpb])
                    RSQ(TMP[2].a[:, HS[h]], pb[:], 1.0, 1e-12, [tpb], [TMP[2].t(h)])
                for h in H2:
                    TT(KH.a[:, HS[h]], TMP[0].a[:, HS[h]], TMP[2].a[:, HS[h]], ALU.mult, [TMP[0].t(h), TMP[2].t(h)], [KH.t(h)])
                for d in range(2):
                    SIG, CUM, EX, AA = TMP
                    hs = lambda bf, h: bf.a[:, HS[h]]
                    for h in H2:
                        pb, tpb = ps_rot()
                        MM(pb[:], DU.a[0:64, l, d, c * 128:(c + 1) * 128], LWLA.a[0:64, HS[h]], [DU.t(), LWLA.t(h)], [tpb])
                        ACT(hs(SIG, h), pb[:], AF.Sigmoid, [tpb, tV], [SIG.t(h)], bias=vcol(i_w0(l, d, c)))
                        pb, tpb = ps_rot()
                        MM(pb[:], DU.a[64:128, l, d, c * 128:(c + 1) * 128], LWLA.a[64:128, HS[h]], [DU.t(), LWLA.t(h)], [tpb])
                        ACT(hs(AA, h), pb[:], AF.Sigmoid, [tpb, tV], [AA.t(h)], bias=vcol(i_a0(l, d, c)))
                    for h in H2:
                        S.op("dve", [RESET.t(), SIG.t(h)], [CUM.t(h)], lambda e: e.tensor_tensor_scan(
                            out=hs(CUM, h), data0=RESET.a[:, HS[h]], data1=hs(SIG, h), initial=0.0, op0=ALU.mult, op1=ALU.add))
                    cv3 = lambda h: hs(CUM, h).rearrange("p (a b) -> p a b", b=128)
                    toth = lambda h: TOT.a[:, 4 * h:4 * h + 4]
                    for h in H2:
                        CP(toth(h).unsqueeze(2), cv3(h)[:, :, 127:128], [CUM.t(h)], [TOT.t(h)])
                    for h in H2:
                        ACT(PC.a[:, d, 4 * h:4 * h + 4], toth(h), AF.Exp, [TOT.t(h)], [PC.t(h)], scale=-LAM)
                    for h in H2:
                        if d == 0:
                            TT(hs(EX, h), hs(CUM, h), hs(SIG, h), ALU.subtract, [CUM.t(h), SIG.t(h)], [EX.t(h)])
                        else:
                            TT(hs(EX, h).rearrange("p (a b) -> p a b", b=128), toth(h).unsqueeze(2).to_broadcast([128, 4, 128]), cv3(h),
                               ALU.subtract, [TOT.t(h), CUM.t(h)], [EX.t(h)])
                    if d == 1:
                        for h in H2:
                            TT(hs(CUM, h), hs(EX, h), hs(SIG, h), ALU.add, [EX.t(h), SIG.t(h)], [CUM.t(h)])
                    for h in H2:
                        ACT(hs(EX, h), hs(EX, h), AF.Exp, [EX.t(h)], [EX.t(h)], scale=-LAM)
                    for h in H2:
                        TT(hs(KHT[d], h), hs(KH, h), hs(EX, h), ALU.mult, [KH.t(h), EX.t(h)], [KHT[d].t(h)])
                    for h in H2:
                        ACT(hs(EX, h), hs(CUM, h), AF.Exp, [CUM.t(h)], [EX.t(h)], scale=-LAM)
                    for h in H2:
                        TT(hs(RT[d], h), hs(R_, h), hs(EX, h), ALU.mult, [R_.t(h), EX.t(h)], [RT[d].t(h)])
                    for h in H2:
                        ACT(hs(CUM, h), hs(CUM, h), AF.Exp, [CUM.t(h)], [CUM.t(h)], scale=LAM)
                    for h in H2:
                        ACT(hs(SIG, h), hs(AA, h), AF.Identity, [AA.t(h), tV, tD], [SIG.t(h)], scale=vcol(i_ka(l, c)), bias=dcol(D_OMKA + l * 4 + c))
                    for h in H2:
                        TT(hs(SIG, h), hs(SIG, h), hs(K_, h), ALU.mult, [SIG.t(h), K_.t(h)], [SIG.t(h)])
                    for h in H2:
                        if d == 0:
                            CP(hs(KTS, h), hs(SIG, h), [SIG.t(h)], [KTS.t(h)])
                        else:
                            TT(hs(KTS, h), hs(KTS, h), hs(SIG, h), ALU.add, [SIG.t(h), KTS.t(h)], [KTS.t(h)])
                    for h in H2:
                        TT(hs(KTT[d], h), hs(SIG, h), hs(CUM, h), ALU.mult, [SIG.t(h), CUM.t(h)], [KTT[d].t(h)])
                    for h in H2:
                        TT(hs(AA, h), hs(AA, h), hs(KH, h), ALU.mult, [AA.t(h), KH.t(h)], [AA.t(h)])
                    for h in H2:
                        STT(hs(BN[d], h), hs(AA, h), -1.0, hs(CUM, h), ALU.mult, ALU.mult, [AA.t(h), CUM.t(h)], [BN[d].t(h)])
                for h in H2:
                    STT(TMP[0].a[:, HS[h]], KTS.a[:, HS[h]], vcol(i_rk(l, c)), R_.a[:, HS[h]], ALU.mult, ALU.mult, [KTS.t(h), tV, R_.t(h)], [TMP[0].t(h)])
                for h in H2:
                    pb, tpb = ps_rot()
                    MM(pb[:], BLK, TMP[0].a[:, HS[h]], [tCF, TMP[0].t(h)], [tpb])
                    TT(BON.a[:, HS[h]], pb[:], V_.a[:, HS[h]], ALU.mult, [tpb, V_.t(h)], [BON.t(h)])
                for h in H2:
                    pb, tpb = ps_rot()
                    MM(pb[:], GU.a[:, l, c * 128:(c + 1) * 128], SLG.a[:, HS[h]], [GU.t(), SLG.t(h)], [tpb])
                    EV(GATE.a[:, HS[h]], pb[:], [tpb], [GATE.t(h)])
                if dbg and stop == f"prep{c}":
                    for nm, bf in (("r", R_), ("k", K_), ("v", V_), ("kh", KH), ("bon", BON)):
                        dump(nm, bf.a[:], allh(bf), [128, NTOK])
                    for d in range(2):
                        for nm, bf in (("rt", RT), ("ktt", KTT), ("bn", BN), ("kht", KHT)):
                            dump(f"{nm}{d}", bf[d].a[:], allh(bf[d]), [128, NTOK], BF16)
                    dump("pc", PC.a[:].rearrange("p d t -> p (d t)"), allh(PC), [128, 16])
                    dump("gate", GATE.a[:], allh(GATE), [128, NTOK], BF16)
                    return True
                return False

            def groups_(c):
                if u == 0:
                    S.op("dve", [], [G32.t()], lambda e: e.memset(G32.a[:], 0.0))
                    S.op("dve", [], [GB.t()], lambda e: e.memset(GB.a[:], 0.0))
                else:
                    for d in range(2):
                        for hh in range(2):
                            S.dma("sp", ST.a[0:64, d, hh * 64:(hh + 1) * 64], st_d[l, d, 2 * c + hh, :, :], [], [ST.t()])
                    pb, tpb = ps_rot()
                    for d in range(2):
                        MM(pb[:, d * 64:(d + 1) * 64], ST.a[:, d, :], IDENT[:, 0:64], [ST.t(), tCF], [tpb])
                    CP(G32.a[:, 0, :], pb[:, 0:128], [tpb], [G32.t()])
                    CP(GB.a[:, 0, :], G32.a[:, 0, :], [G32.t()], [GB.t()])
                if dbg and stop == "st":
                    dump("g32", G32.a[:, 0, :], G32.t(), [128, 128])
                    return True
                yinit = set()
                with ExitStack() as gg:
                    NSETS, KPRE = 5, 3
                    CS = [dict(TM=mk("TM", [128, 8, 128], BF16, gg), AbT=mk("G_AbT", [128, 512], BF16, gg), AkT=mk("G_AkT", [128, 512], BF16, gg),
                               WkT=mk("G_WkT", [128, 512], BF16, gg), KHP=mk("KHP", [128, 256], BF16, gg), UU=mk("UU", [128, 256], BF16, gg)) for _ in range(NSETS)]
                    PS_ = [dict(N=mk("G_N", [128, 512], BF16, gg), Lk=mk("G_Lk", [128, 512], BF16, gg), Z=mk("G_Z", [128, 512], BF16, gg),
                                FT=mk("G_FT", [128, 512], BF16, gg), FF=[mk(f"FF{i}", [128, 512], BF16, gg) for i in range(2)]) for _ in range(KPRE)]
                    cset = {}
                    groups = [(b, i) for b in range(B) for i in range(nT)]
                    TIs = {}

                    def tl_of(b, i):
                        return [b * nT + i, b * nT + nT - 1 - i]

                    def bank_split_gram(LB, RB, tl, add_ident=False):
                        pbs = [ps_rot(), ps_rot()]
                        for hh in range(2):
                            pr = slice(hh * 64, (hh + 1) * 64)
                            for d in range(2):
                                o = pbs[hh][0][:, d * 128:(d + 1) * 128]
                                MM(o, LB[d].a[pr, tl[d] * 128:(tl[d] + 1) * 128], RB[d].a[pr, tl[d] * 128:(tl[d] + 1) * 128],
                                   [LB[d].t(tl[d] // 4), RB[d].t(tl[d] // 4)], [pbs[hh][1]], start=True, stop=not add_ident)
                                if add_ident:
                                    MM(o, IDB, IDB, [tCB], [pbs[hh][1]], start=False, stop=True)
                        return pbs

                    def pre_gen(g, pslot):
                        b, i = groups[g]
                        tl = tl_of(b, i)
                        C_ = CS[cset[g]]
                        P_ = PS_[pslot]
                        TM = C_["TM"]
                        srcs = [(KHT[0], tl[0]), (KTT[0], tl[0]), (BN[0], tl[0]), (KHT[1], tl[1]), (KTT[1], tl[1]), (BN[1], tl[1]), (VB, tl[0]), (VB, tl[1])]
                        for g2 in range(2):
                            pb, tpb = ps_rot()
                            for k in range(4):
                                bf, tt = srcs[g2 * 4 + k]
                                MM(pb[:, k * 128:(k + 1) * 128], tile_(bf, tt), IDB, [bf.t(tt // 4), tCB], [tpb])
                            EV(TM.a[:, g2 * 4:(g2 + 1) * 4, :], pb[:].rearrange("p (k f) -> p k f", f=128), [tpb], [TM.t()])
                        yield
                        pbs = bank_split_gram(KHT, BN, tl)
                        gv = P_["N"].a[:].rearrange("p (d h t) -> p d h t", d=2, h=2)
                        for hh in range(2):
                            CP(gv[:, :, hh, :], pbs[hh][0][:, 0:256].rearrange("p (d t) -> p d t", d=2), [pbs[hh][1]], [P_["N"].t()], eng="act")
                        yield
                        for nm, LB, RB, MSK, dstb, addi in (("NT", BN, KHT, LMASK[0], P_["FF"][0], False), ("Lk", KHT, KTT, MASKA, P_["Lk"], False),
                                                          ("AbT", BN, RT, MASKC, C_["AbT"], False), ("AkT", KTT, RT, MASKC, C_["AkT"], False)):
                            pbs = bank_split_gram(LB, RB, tl, add_ident=addi)
                            gv = dstb.a[:].rearrange("p (d h t) -> p d h t", d=2, h=2)
                            mv = MSK.rearrange("p (d h t) -> p d h t", d=2, h=2)
                            for hh in range(2):
                                TT(gv[:, :, hh, :], pbs[hh][0][:, 0:256].rearrange("p (d t) -> p d t", d=2), mv[:, :, hh, :], ALU.mult,
                                   [pbs[hh][1], tCB], [dstb.t()])
                            yield
                        fi = 0
                        FF = P_["FF"]
                        for lv in range(1, 7):
                            Fc, Fn = FF[fi], FF[1 - fi]
                            first = (lv == 1)
                            pb, tpb = ps_rot()
                            for q in range(4):
                                qs = slice(q * 128, (q + 1) * 128)
                                MM(pb[:, qs], Fc.a[:, qs], IDB, [Fc.t(), tCB], [tpb], start=True, stop=not first)
                                if first:
                                    MM(pb[:, qs], IDB, IDB, [tCB], [tpb], start=False, stop=True)
                            CP(P_["FT"].a[:], pb[:], [tpb], [P_["FT"].t()], eng="act")
                            pb, tpb = ps_rot()
                            for q in range(4):
                                qs = slice(q * 128, (q + 1) * 128)
                                MM(pb[:, qs], P_["N"].a[:, qs], Fc.a[:, qs], [P_["N"].t(), Fc.t()], [tpb], start=True, stop=not first)
                                if first:
                                    MM(pb[:, qs], P_["N"].a[:, qs], IDB, [P_["N"].t(), tCB], [tpb], start=False, stop=True)
                            TT(P_["Z"].a[:], pb[:], LMASK[lv], ALU.mult, [tpb, tCB], [P_["Z"].t()])
                            yield
                            on_dve = lv in (2, 5)
                            pb, tpb = ps_rot()
                            for q in range(4):
                                qs = slice(q * 128, (q + 1) * 128)
                                MM(pb[:, qs], P_["FT"].a[:, qs], P_["Z"].a[:, qs], [P_["FT"].t(), P_["Z"].t()], [tpb], start=True, stop=on_dve)
                                if not on_dve:
                                    MM(pb[:, qs], IDB, Fc.a[:, qs], [tCB, Fc.t()], [tpb], start=False, stop=not first)
                                    if first:
                                        MM(pb[:, qs], IDB, IDB, [tCB], [tpb], start=False, stop=True)
                            if on_dve:
                                TT(Fn.a[:], pb[:], Fc.a[:], ALU.add, [tpb, Fc.t()], [Fn.t()])
                            else:
                                CP(Fn.a[:], pb[:], [tpb], [Fn.t()], eng="act")
                            fi = 1 - fi
                            yield
                        TI = FF[fi]
                        tm = lambda k, hh: TM.a[:, k, hh * 64:(hh + 1) * 64]
                        pb, tpb = ps_rot()
                        for hh in range(2):
                            for d in range(2):
                                q = d * 2 + hh
                                MM(pb[hh * 64:(hh + 1) * 64, d * 128:(d + 1) * 128], tm(d * 3, hh), TI.a[:, q * 128:(q + 1) * 128], [TM.t(), TI.t()], [tpb])
                        CP(C_["KHP"].a[:], pb[:, 0:256], [tpb], [C_["KHP"].t()], eng="act")
                        pb, tpb = ps_rot()
                        for q in range(4):
                            qs = slice(q * 128, (q + 1) * 128)
                            MM(pb[:, qs], P_["Lk"].a[:, qs], TI.a[:, qs], [P_["Lk"].t(), TI.t()], [tpb])
                        CP(C_["WkT"].a[:], pb[:], [tpb], [C_["WkT"].t()])
                        yield

                    def chain_gen(g):
                        b, i = groups[g]
                        tl = tl_of(b, i)
                        C_ = CS[cset[g]]
                        TM, KHP, UU = C_["TM"], C_["KHP"], C_["UU"]
                        tm = lambda k, hh: TM.a[:, k, hh * 64:(hh + 1) * 64]
                        gb = lambda d, hh: GB.a[hh * 64:(hh + 1) * 64, b, d * 64:(d + 1) * 64]
                        pu, tpu = PSB[6]
                        for hh in range(2):
                            for d in range(2):
                                q = d * 2 + hh
                                MM(pu[:, q * 64:(q + 1) * 64], KHP.a[hh * 64:(hh + 1) * 64, d * 128:(d + 1) * 128], gb(d, hh), [KHP.t(), GB.t()], [tpu], start=True, stop=False)
                                MM(pu[:, q * 64:(q + 1) * 64], C_["WkT"].a[:, q * 128:(q + 1) * 128], tm(6 + d, hh), [C_["WkT"].t(), TM.t()], [tpu], start=False, stop=True)
                        CP(UU.a[:], pu[:, 0:256], [tpu], [UU.t()], eng="act")
                        yield
                        py, tpy = PSB[7]
                        for hh in range(2):
                            for d in range(2):
                                q = d * 2 + hh
                                pr = slice(hh * 64, (hh + 1) * 64)
                                o = py[pr, d * 128:(d + 1) * 128]
                                MM(o, gb(d, hh), RT[d].a[pr, tl[d] * 128:(tl[d] + 1) * 128], [GB.t(), RT[d].t(tl[d] // 4)], [tpy], start=True, stop=False)
                                MM(o, UU.a[:, q * 64:(q + 1) * 64], C_["AbT"].a[:, q * 128:(q + 1) * 128], [UU.t(), C_["AbT"].t()], [tpy], start=False, stop=False)
                                MM(o, tm(6 + d, hh), C_["AkT"].a[:, q * 128:(q + 1) * 128], [TM.t(), C_["AkT"].t()], [tpy], start=False, stop=True)
                        for d in range(2):
                            dst = tile_(YACC, tl[d])
                            if tl[d] in yinit:
                                TT(dst, py[:, d * 128:(d + 1) * 128], dst, ALU.add, [tpy, YACC.t(tl[d] // 4)], [YACC.t(tl[d] // 4)])
                            else:
                                yinit.add(tl[d])
                                CP(dst, py[:, d * 128:(d + 1) * 128], [tpy], [YACC.t(tl[d] // 4)], eng="act")
                        if not (u == 1 and i == nT - 1):
                            pg, tpg = PSB[6]
                            for hh in range(2):
                                for d in range(2):
                                    q = d * 2 + hh
                                    o = pg[hh * 64:(hh + 1) * 64, 256 + d * 64:256 + (d + 1) * 64]
                                    MM(o, tm(d * 3 + 1, hh), tm(6 + d, hh), [TM.t()], [tpg], start=True, stop=False)
                                    MM(o, tm(d * 3 + 2, hh), UU.a[:, q * 64:(q + 1) * 64], [TM.t(), UU.t()], [tpg], start=False, stop=True)
                            TT(GT.a[:], pg[:, 256:384], G32.a[:, b, :], ALU.add, [tpg, G32.t()], [GT.t()])
                            for d in range(2):
                                TS(G32.a[:, b, d * 64:(d + 1) * 64], GT.a[:, d * 64:(d + 1) * 64], PC.a[:, d, tl[d]:tl[d] + 1], None, ALU.mult, None, [GT.t(), PC.t(tl[d] // 4)], [G32.t()])
                            CP(GB.a[:, b, :], G32.a[:, b, :], [G32.t()], [GB.t()], eng="act")
                        if u == 0 and i == nT - 1:
                            pb, tpb = ps_rot()
                            for d in range(2):
                                MM(pb[0:64, d * 128:(d + 1) * 128], G32.a[:, b, d * 64:(d + 1) * 64], IDENT, [G32.t(), tCF], [tpb])
                            CP(SO.a[:].rearrange("v d k -> v (d k)"), pb[0:64, 0:256], [tpb], [SO.t()])
                            for d in range(2):
                                out_deps.append(S.dma("sp", nst_d[b, l, d, 2 * c:2 * c + 2, :, :].rearrange("h v k -> v h k"),
                                                      SO.a[:, d, :].rearrange("v (h k) -> v h k", k=64), [SO.t()], []))
                        yield

                    ng = len(groups)
                    free_sets = list(range(NSETS))
                    free_pslots = list(range(KPRE))
                    active = []
                    ready = set()
                    nxt_pre = 0
                    nxt_chain = 0
                    cgen = None
                    rnd = 0
                    last_start = -100
                    while nxt_chain < ng:
                        rnd += 1
                        if nxt_pre < ng and free_sets and free_pslots and rnd - last_start >= 1:
                            cset[nxt_pre] = free_sets.pop(0)
                            ps_ = free_pslots.pop(0)
                            active.append([pre_gen(nxt_pre, ps_), nxt_pre, ps_])
                            nxt_pre += 1
                            last_start = rnd
                        for ent in list(active):
                            try:
                                next(ent[0])
                            except StopIteration:
                                active.remove(ent)
                                ready.add(ent[1])
                                free_pslots.append(ent[2])
                        if cgen is None and nxt_chain in ready:
                            cgen = chain_gen(nxt_chain)
                        if cgen is not None:
                            try:
                                next(cgen)
                            except StopIteration:
                                free_sets.append(cset[nxt_chain])
                                cgen = None
                                nxt_chain += 1
                    S.barrier()
                return False

            def output_(c, oo):
                MU = f32b("MU", oo)
                T1 = f32b("T1", oo)
                T2 = f32b("T2", oo)
                H2 = range(2)
                hs = lambda bf, h: bf.a[:, HS[h]]
                for h in H2:
                    pb, tpb = ps_rot()
                    MM(pb[:], BLK, hs(YACC, h), [tCF, YACC.t(h)], [tpb])
                    STT(hs(MU, h), pb[:], -1.0 / 64, hs(YACC, h), ALU.mult, ALU.add, [tpb, YACC.t(h)], [MU.t(h)])
                for h in H2:
                    ACT(hs(T1, h), hs(MU, h), AF.Square, [MU.t(h)], [T1.t(h)])
                for h in H2:
                    pb, tpb = ps_rot()
                    MM(pb[:], BLK, hs(T1, h), [tCF, T1.t(h)], [tpb])
                    RSQ(hs(T2, h), pb[:], 1.0 / 64, GN_EPS, [tpb], [T2.t(h)])
                for h in H2:
                    TT(hs(MU, h), hs(MU, h), hs(T2, h), ALU.mult, [MU.t(h), T2.t(h)], [MU.t(h)])
                for h in H2:
                    ACT(hs(MU, h), hs(MU, h), AF.Identity, [MU.t(h), tV], [MU.t(h)], scale=vcol(i_gnw(l, c)), bias=vcol(i_gnb(l, c)))
                for h in H2:
                    TT(hs(MU, h), hs(MU, h), hs(BON, h), ALU.add, [MU.t(h), BON.t(h)], [MU.t(h)])
                for h in H2:
                    TT(OR_.a[:, c, HS[h]], hs(MU, h), hs(GATE, h), ALU.mult, [MU.t(h), GATE.t(h)], [OR_.t(c)])


            pp = ExitStack()
            bufs = prepA(0, pp)
            if prepB(0, *bufs):
                return True
            S.barrier()
            pp.close()
            for c in range(4):
                if dbg and stop == "bar":
                    dump("gate", GATE.a[:], [GATE.t(0), GATE.t(1)], [128, NTOK], BF16)
                    return True
                if groups_(c):
                    return True
                if dbg and stop == f"y{c}":
                    dump("yacc", YACC.a[:], [YACC.t(0), YACC.t(1)], [128, NTOK])
                    return True
                pp = ExitStack()
                if c < 3:
                    bufs = prepA(c + 1, pp)
                oo = ExitStack()
                output_(c, oo)
                if c < 3:
                    if prepB(c + 1, *bufs):
                        return True
                S.barrier()
                oo.close()
                pp.close()

            if dbg and stop == "rwkv":
                dump("or", OR_.a[:].rearrange("p c t -> p (c t)"), list(OR_.ts.values()), [128, 4 * NTOK], BF16)
                return True
            return False

        def attention(u, l, B, TL, OA_, at):
            NK = TL + (256 if u == 1 else 0)
            KOFF = NK - TL
            nkt = NK // 128
            QT_ = mk("QTb", [128, 4, NTOK], BF16, at)
            K32 = mk("K32", [128, 1280], F32, at)
            KT2 = mk("KT2", [128, 2, 1280], BF16, at)
            VTM = mk("VTM", [128, 10, 128], BF16, at)
            R1 = [mk(f"aR{i}", [128, 512], F32, at) for i in range(2)]
            PT = [mk(f"PT{i}", [128, 512], BF16, at) for i in range(3)]
            RCPS = [mk(f"RCP{i}", [128, 512], F32, at) for i in range(2)]
            if u == 1:
                ROPE = mk("ROPE", [128, 2048], F32, at)
                CK = mk("CKl", [128, 2, 128], F32, at)
                S.dma("sp", ROPE.a[:], rope_d, [], [ROPE.t()])
                S.dma("sp", CK.a[:], ck_d[l].rearrange("(a p) f -> p a f", p=128), [], [CK.t()])
                S.dma("pool", VTM.a[:, 0:2, :], cv_d[l].rearrange("(a p) f -> p a f", p=128), [], [VTM.t()])
                pb, tpb = ps_rot()
                for a in range(2):
                    MM(pb[:, a * 128:(a + 1) * 128], CK.a[:, a, :], IDENT, [CK.t(), tCF], [tpb])
                CP(K32.a[:, 0:256], pb[:, 0:256], [tpb], [K32.t()])
            wq, twq, wiq = ws.pop(f"q{u}{l}")
            wqv = wq[:, 0:4096].rearrange("p (a b) -> p a b", b=512)
            wk, twk, wik = ws.pop(f"kv{u}{l}")
            wkv = wk[:, 0:2048].rearrange("p (a b) -> p a b", b=256)

            Q32 = [mk(f"Q32b{i}", [128, 512], F32, at) for i in range(3)]
            SQ = [mk(f"aSQb{i}", [128, 512], BF16, at) for i in range(3)]
            RSS = [mk(f"aRSb{i}", [128, 512], F32, at) for i in range(3)]
            qitems = [dict(h=h, ch=ch, k=k) for k, (h, ch) in enumerate((h, ch) for h in range(2) for ch in range(5))]

            def qk_s1(it):
                h, ch, k = it["h"], it["ch"], it["k"]
                pb, tpb = ps_rot()
                for kc in range(8):
                    lhs = wqv[:, kc, ch * 128:(ch + 1) * 128] if ch < 4 else wkv[:, kc, 0:128]
                    MM(pb[:], lhs, H.a[:, kc, HS[h]], [twq if ch < 4 else twk, H.t(kc, h)], [tpb], start=(kc == 0), stop=(kc == 7))
                q32 = Q32[k % 3]
                CP(q32.a[:], pb[:], [tpb], [q32.t()], eng="act")
                sq = SQ[k % 3]
                ACT(sq.a[:], q32.a[:], AF.Square, [q32.t()], [sq.t()])

            def qk_s2(it):
                h, ch, k = it["h"], it["ch"], it["k"]
                q32, sq, RS = Q32[k % 3], SQ[k % 3], RSS[k % 3]
                pb2, tpb2 = ps_rot()
                MM(pb2[:], BLK16, sq.a[:], [tCB, sq.t()], [tpb2])
                RSQ(RS.a[:], pb2[:], 1.0 / 64, NORM_EPS, [tpb2], [RS.t()])
                gcol = vcol(i_qg(l)) if ch < 4 else vcol(i_kg(l))
                if u == 0:
                    if ch < 4:
                        STT(QT_.a[:, ch, HS[h]], q32.a[:], gcol, RS.a[:], ALU.mult, ALU.mult, [q32.t(), tV, RS.t()], [QT_.t(ch, h)])
                    else:
                        STT(K32.a[:, h * 512:(h + 1) * 512], q32.a[:], gcol, RS.a[:], ALU.mult, ALU.mult, [q32.t(), tV, RS.t()], [K32.t()])
                else:
                    STT(q32.a[:], q32.a[:], gcol, RS.a[:], ALU.mult, ALU.mult, [q32.t(), tV, RS.t()], [q32.t()])

            def qk_s3(it):
                if u == 0:
                    return
                h, ch, k = it["h"], it["ch"], it["k"]
                q32 = Q32[k % 3]
                pb3, tpb3 = ps_rot()
                MM(pb3[:], ROT, q32.a[:], [tCF, q32.t()], [tpb3])
                r1 = R1[0]
                r2 = R1[1]
                TT(r1.a[:], q32.a[:], ROPE.a[:, h * 512:(h + 1) * 512], ALU.mult, [q32.t(), ROPE.t()], [r1.t()])
                TT(r2.a[:], pb3[:], ROPE.a[:, 1024 + h * 512:1024 + (h + 1) * 512], ALU.mult, [tpb3, ROPE.t()], [r2.t()])
                if ch < 4:
                    TT(QT_.a[:, ch, HS[h]], r1.a[:], r2.a[:], ALU.add, [r1.t(), r2.t()], [QT_.t(ch, h)])
                else:
                    TT(K32.a[:, 256 + h * 512:256 + (h + 1) * 512], r1.a[:], r2.a[:], ALU.add, [r1.t(), r2.t()], [K32.t()])

            nq = len(qitems)
            for t in range(nq + 2):
                if t < nq:
                    qk_s1(qitems[t])
                if 0 <= t - 1 < nq:
                    qk_s2(qitems[t - 1])
                if 0 <= t - 2 < nq:
                    qk_s3(qitems[t - 2])
            for tt in range(8):
                pb, tpb = ps_rot()
                for kc in range(8):
                    MM(pb[:, 0:128], H.a[:, kc, tt * 128:(tt + 1) * 128], wkv[:, kc, 128:256], [H.t(kc, tt // 4), twk], [tpb], start=(kc == 0), stop=(kc == 7))
                if u == 0:
                    CP(VTM.a[:, tt, :], pb[:, 0:128], [tpb], [VTM.t()], eng="act")
                    stg = STG[tt % 2]
                    CP(stg.a[:, 0:128], pb[:, 0:128], [tpb], [stg.t()])
                    out_deps.append(S.dma("sp", ncv_d[tt // 2, l, (tt % 2) * 128:(tt % 2 + 1) * 128, :], stg.a[:, 0:128], [stg.t()], []))
                else:
                    CP(VTM.a[:, 2 + tt, :], pb[:, 0:128], [tpb], [VTM.t()], eng="act")
            ws.rel(wiq)
            ws.rel(wik)
            if u == 0:
                for tt in range(8):
                    pb, tpb = ps_rot()
                    MM(pb[:, 0:128], K32.a[:, tt * 128:(tt + 1) * 128], IDENT, [K32.t(), tCF], [tpb])
                    stg = STG[tt % 2]
                    CP(stg.a[:, 0:128], pb[:, 0:128], [tpb], [stg.t()])
                    out_deps.append(S.dma("sp", nck_d[tt // 2, l, (tt % 2) * 128:(tt % 2 + 1) * 128, :], stg.a[:, 0:128], [stg.t()], []))
            ncol = 1024 if u == 0 else 1280
            for kvh in range(2):
                for c0 in range(0, ncol, 512):
                    w = min(512, ncol - c0)
                    pb, tpb = ps_rot()
                    MM(pb[:, 0:w], SEL[kvh], K32.a[:, c0:c0 + w], [tCF, K32.t()], [tpb])
                    EV(KT2.a[:, kvh, c0:c0 + w], pb[:, 0:w], [tpb], [KT2.t()])
            if dbg and stop == "qk":
                dump("qt", QT_.a[:].rearrange("p c t -> p (c t)"), list(QT_.ts.values()), [128, 4 * NTOK], BF16)
                dump("kt2", KT2.a[:].rearrange("p c t -> p (c t)"), KT2.t(), [128, 2 * 1280], BF16)
                dump("vtm", VTM.a[:].rearrange("p c t -> p (c t)"), VTM.t(), [128, 1280], BF16)
                return True
            QB = 512 if u == 1 else 256
            items = []
            seti = 0
            for b in range(B):
                for qb in range(TL // QB):
                    q0 = b * TL + qb * QB
                    for qc in range(4):
                        bank = 4 + 2 * (seti % 2)
                        rcp = RCPS[seti % 2]
                        seti += 1
                        for hh in range(2):
                            for kt in range(nkt):
                                items.append(dict(b=b, q0=q0, qc=qc, hh=hh, kt=kt, bank=bank, rcp=rcp,
                                                  last=(hh == 1 and kt == nkt - 1)))
            pti = [0]

            def emit_score(it):
                hh, qc, kt, b, q0 = it["hh"], it["qc"], it["kt"], it["b"], it["q0"]
                kvh = (qc * 2 + hh) // 4
                pr = slice(hh * 64, (hh + 1) * 64)
                kcol = (b * TL if u == 0 else 0) + kt * 128
                psx, tpsx = ps_rot(0, 4)
                MM(psx[:, 0:QB], KT2.a[pr, kvh, kcol:kcol + 128], QT_.a[pr, qc, q0:q0 + QB], [KT2.t(), QT_.t(qc, q0 // 512)], [tpsx])
                pt = PT[pti[0] % 3]
                pti[0] += 1
                ACT(pt.a[:, 0:QB], psx[:, 0:QB], AF.Exp, [tpsx], [pt.t()], scale=0.125)
                it["pt"] = pt

            def emit_pv(it):
                hh, qc, kt, b, q0 = it["hh"], it["qc"], it["kt"], it["b"], it["q0"]
                kvh = (qc * 2 + hh) // 4
                pr = slice(hh * 64, (hh + 1) * 64)
                vt_i = (b * 2 + kt) if u == 0 else kt
                po, tpo = PSB[it["bank"]]
                pr_, tpr = PSB[it["bank"] + 1]
                pt = it["pt"]
                MM(po[pr, 0:QB], VTM.a[:, vt_i, kvh * 64:(kvh + 1) * 64], pt.a[:, 0:QB], [VTM.t(), pt.t()], [tpo], start=(kt == 0), stop=(kt == nkt - 1))
                MM(pr_[pr, 0:QB], ONESB, pt.a[:, 0:QB], [tCB, pt.t()], [tpr], start=(kt == 0), stop=(kt == nkt - 1))
                if it["last"]:
                    rcp = it["rcp"]
                    S.op("act", [tpr], [rcp.t()], lambda e: e.activation(out=rcp.a[:, 0:QB], in_=pr_[:, 0:QB], func=AF.Ln))
                    S.op("act", [rcp.t()], [rcp.t()], lambda e: e.activation(out=rcp.a[:, 0:QB], in_=rcp.a[:, 0:QB], func=AF.Exp, scale=-1.0))
                    TT(OA_.a[:, qc, q0:q0 + QB], po[:, 0:QB], rcp.a[:, 0:QB], ALU.mult, [tpo, rcp.t()], [OA_.t(qc)])

            LOOK = 2
            for i in range(min(LOOK, len(items))):
                emit_score(items[i])
            for i in range(len(items)):
                if i + LOOK < len(items):
                    emit_score(items[i + LOOK])
                emit_pv(items[i])
            if dbg and stop == "attn":
                dump("oa", OA_.a[:].rearrange("p c t -> p (c t)"), list(OA_.ts.values()), [128, 4 * NTOK], BF16)
                return True
            return False

        def merge(u, l, OR_, OA_, mg):
            MG = mk("MG", [128, 8, NTOK], BF16, mg)
            M32 = mk("M32", [128, 8, 512], F32, mg)
            WBR = mk("WBR", [128, 8, 1024], BF16, mg)
            SG = [mk(f"SG{i}", [128, 512], F32, mg) for i in range(2)]
            TQ = [mk(f"TQ{i}", [128, 512], F32, mg) for i in range(2)]
            SQ = [mk(f"mSQ{i}", [128, 512], BF16, mg) for i in range(2)]
            RS = mk("mRS", [128, 512], F32, mg)
            TMP = [mk(f"mTMP{i}", [128, 512], F32, mg) for i in range(2)]
            S.dma("pool", WBR.a[:], w_br[l], [], [WBR.t()])
            for jj in range(2):
                gr, tgr, wir = ws.pop(f"g{u}{l}{jj}")
                ga, tga, wia = ws.pop(f"g{u}{l}{2 + jj}")
                gw = [gr[:, 0:4096].rearrange("p (a b) -> p a b", b=512), ga[:, 0:4096].rearrange("p (a b) -> p a b", b=512)]
                tg = [tgr, tga]
                for j4 in range(4):
                    j = jj * 4 + j4
                    for h in range(2):
                        for i in range(2):
                            pg, tpg = ps_rot()
                            for kc in range(8):
                                MM(pg[:], gw[i][:, kc, j4 * 128:(j4 + 1) * 128], H.a[:, kc, HS[h]], [tg[i], H.t(kc, h)], [tpg], start=(kc == 0), stop=(kc == 7))
                            ACT(SG[i].a[:], pg[:], AF.Sigmoid, [tpg], [SG[i].t()])
                            pbr, tpbr = ps_rot()
                            SRC = OR_ if i == 0 else OA_
                            for kc in range(4):
                                MM(pbr[:], WBR.a[:, i * 4 + kc, j * 128:(j + 1) * 128], SRC.a[:, kc, HS[h]], [WBR.t(), SRC.t(kc)], [tpbr], start=(kc == 0), stop=(kc == 3))
                            TT(TQ[i].a[:], pbr[:], SG[i].a[:], ALU.mult, [tpbr, SG[i].t()], [TQ[i].t()])
                        TT(MG.a[:, j, HS[h]], TQ[0].a[:], TQ[1].a[:], ALU.add, [TQ[0].t(), TQ[1].t()], [MG.t(j, h)])
                ws.rel(wir)
                ws.rel(wia)
            if dbg and stop == "mg":
                dump("mg", MG.a[:].rearrange("p c t -> p (c t)"), list(MG.ts.values()), [128, 8 * NTOK], BF16)
                return True
            wo = [ws.pop(f"wo{u}{l}{jt}") for jt in range(2)]
            for h in range(2):
                for j in range(8):
                    wsl, twsl, _ = wo[j // 4]
                    wv = wsl[:, 0:4096].rearrange("p (a b) -> p a b", b=512)
                    pb, tpb = ps_rot()
                    for kc in range(8):
                        MM(pb[:], wv[:, kc, (j % 4) * 128:(j % 4 + 1) * 128], MG.a[:, kc, HS[h]], [twsl, MG.t(kc, h)], [tpb], start=(kc == 0), stop=(kc == 7))
                    EV(M32.a[:, j, :], pb[:], [tpb], [M32.t(j)])
                resid_add(u, l, M32, D_G1, h, (SQ, RS, TMP))
            ws.rel(wo[0][2])
            ws.rel(wo[1][2])
            return False

        def mlp(u, l, ml):
            UB = mk("UB", [128, 32, NTOK], BF16, ml)
            FB = mk("FB", [128, 8, 512], F32, ml)
            RL = [mk("RL0", [128, 512], F32, ml)] * 2
            SQ = [mk("fSQ0", [128, 512], BF16, ml)] * 2
            RS = mk("fRS", [128, 512], F32, ml)
            TMP = [mk("fTMP0", [128, 512], F32, ml)] * 2
            k = 0
            for tb in range(8):
                wsl, twsl, wi = ws.pop(f"up{u}{l}{tb}")
                wv = wsl[:, 0:4096].rearrange("p (a b) -> p a b", b=512)
                for h in range(2):
                    for f4 in range(4):
                        fc = tb * 4 + f4
                        pb, tpb = ps_rot()
                        for kc in range(8):
                            MM(pb[:], wv[:, kc, f4 * 128:(f4 + 1) * 128], H.a[:, kc, HS[h]], [twsl, H.t(kc, h)], [tpb], start=(kc == 0), stop=(kc == 7))
                        rl = RL[k % 2]
                        k += 1
                        ACT(rl.a[:], pb[:], AF.Relu, [tpb], [rl.t()])
                        TT(UB.a[:, fc, HS[h]], rl.a[:], rl.a[:], ALU.mult, [rl.t()], [UB.t(fc, h)])
                ws.rel(wi)
            FB1 = mk("FB1", [128, 8, 512], F32, ml)
            FBS = [FB, FB1]
            for j in range(8):
                wsl, twsl, wi = ws.pop(f"dn{u}{l}{j}")
                wv = wsl[:, 0:4096].rearrange("p (a b) -> p a b", b=128)
                for h in range(2):
                    pb, tpb = ps_rot()
                    for fc in range(32):
                        MM(pb[:], wv[:, fc, :], UB.a[:, fc, HS[h]], [twsl, UB.t(fc, h)], [tpb], start=(fc == 0), stop=(fc == 31))
                    EV(FBS[h].a[:, j, :], pb[:], [tpb], [FBS[h].t(j)])
                ws.rel(wi)
            for h in range(2):
                resid_add(u, l, FBS[h], D_G2, h, (SQ, RS, TMP))
            return False

        done = False
        for u in units:
            load_x(u)
            S.barrier()
            for l in layers:
                if layer(u, l):
                    done = True
                    break
            if done:
                break
            store_y(u)
        S.finish(out_deps)
        build.stats = dict(cnt=dict(S.cnt), nsem=S.nsem, nwait=S.nwait)
    return nc, dumps


def _consts():
    p = np.arange(128)
    ident = np.eye(128, dtype=np.float32)
    blk = (p[:, None] // 64 == p[None, :] // 64).astype(np.float32)
    ones = np.ones((128, 128), np.float32)
    rot = np.zeros((128, 128), np.float32)
    for m in range(128):
        d = m % 64
        half = (d % 32) // 16
        if half == 0:
            rot[m + 16, m] = -1.0
        else:
            rot[m - 16, m] = 1.0
    sel = []
    for kvh in range(2):
        s = np.zeros((128, 128), np.float32)
        for m in range(128):
            s[kvh * 64 + m % 64, m] = 1.0
        sel.append(s)
    cstf = np.concatenate([ident, blk, ones, rot, sel[0], sel[1]], axis=1)
    j = p[None, :]
    pp = p[:, None]
    sl = (j < pp).astype(np.float32)
    su = (j > pp).astype(np.float32)
    il = (j <= pp).astype(np.float32)
    iu = (j >= pp).astype(np.float32)
    maska = np.concatenate([sl, sl, su, su], 1)
    maskb = np.concatenate([su, su, sl, sl], 1)
    maskc = np.concatenate([iu, iu, il, il], 1)
    lms = []
    for lv in range(7):
        s_ = 1 << lv
        m = ((pp // (2 * s_) == j // (2 * s_)) & (pp % (2 * s_) < s_) & (j % (2 * s_) >= s_)).astype(np.float32)
        lms.append(np.concatenate([m, m, m.T, m.T], 1))
    cstb = np.concatenate([maska, maskb, maskc, ident, np.ones((128, 64), np.float32)] + lms + [ones, blk], axis=1)
    t = np.arange(1024)
    d = p % 64
    axis = d // 32
    f = d % 16
    freqs = (1.0 / (10000.0 ** (np.arange(0, 32, 2, dtype=np.float32) / 32.0))).astype(np.float32)
    pos = np.where(axis[:, None] == 0, (t // 64)[None, :], (t % 64)[None, :]).astype(np.float32)
    ang = pos * freqs[f][:, None]
    rope = np.concatenate([np.cos(ang), np.sin(ang)], axis=1).astype(np.float32)
    return np.ascontiguousarray(cstf), np.ascontiguousarray(cstb), np.ascontiguousarray(rope)


def _vtable(inp, core):
    rows = np.zeros((NV, 128), np.float32)
    r = lambda a: np.asarray(a, np.float32).reshape(-1, 128)
    rows[0:64] = r(inp["norm_g"])
    rows[64:160] = r(inp["b_mod"])
    rows[160:184] = r(inp["rwkv_mu"])
    rows[184:192] = r(inp["rwkv_k_k"])
    rows[192:200] = r(inp["rwkv_k_a"])
    rows[200:208] = r(inp["rwkv_r_k"])
    rows[208:224] = r(inp["decay_w0"])
    rows[224:240] = r(inp["iclr_a0"])
    rows[240:248] = r(inp["gn_w"])
    rows[248:256] = r(inp["gn_b"])
    rows[256:258] = np.tile(np.asarray(inp["q_gain"], np.float32), (1, 2))
    rows[258:260] = np.tile(np.asarray(inp["k_gain"], np.float32), (1, 2))
    rows[260:268] = r(inp["c_ctx"])
    rows[268:276] = r(np.asarray(inp["c"])[core // 2])
    return rows


_CACHE = {}


def _get_nc(**kw):
    key = tuple(sorted((k, str(v)) for k, v in kw.items()))
    if key not in _CACHE:
        _CACHE[key] = build(**kw)
    return _CACHE[key]


def make_in_maps(inp):
    cstf, cstb, rope = _consts()
    f = lambda a: np.ascontiguousarray(np.asarray(a, np.float32))
    shared = {k: f(inp[k]) for k in ("w_in", "w_br", "w_out", "w_mod", "mlp_up", "mlp_down", "decay_up", "iclr_up", "gate_up")}
    xp = f(inp["x_prompt"])
    xs = f(inp["x_sample"])
    ck = f(inp["cache_k"])
    cv = f(inp["cache_v"])
    st = f(inp["state_wkv"])
    maps = []
    for i in range(8):
        b = i // 2
        m = dict(shared)
        m["xp"] = np.ascontiguousarray(xp[4 * i:4 * i + 4].reshape(NTOK, DM))
        m["xs"] = np.ascontiguousarray(xs[b])
        m["ck"] = np.ascontiguousarray(ck[b].reshape(NL, 256, 128))
        m["cv"] = np.ascontiguousarray(cv[b].reshape(NL, 256, 128))
        m["st"] = np.ascontiguousarray(st[b])
        m["vt"] = _vtable(inp, i)
        m["cstf"] = cstf
        m["cstb"] = cstb
        m["rope"] = rope
        maps.append(m)
    return maps


def kernel(**inputs):
    nc, _ = _get_nc()
    maps = make_in_maps(inputs)
    res = run_bass_kernel_spmd(nc, maps, core_ids=list(range(8)))
    R = res.results
    y_prompt = np.concatenate([R[i]["yp"].reshape(4, 256, DM) for i in range(8)], axis=0).astype(np.float32)
    y_sample = np.stack([R[2 * b]["ys"] for b in range(4)], axis=0).astype(np.float32)
    nck = np.concatenate([R[i]["nck"].reshape(4, NL, 256, 2, 64) for i in range(8)], axis=0).astype(np.float32)
    ncv = np.concatenate([R[i]["ncv"].reshape(4, NL, 256, 2, 64) for i in range(8)], axis=0).astype(np.float32)
    nst = np.concatenate([R[i]["nst"] for i in range(8)], axis=0).astype(np.float32)
    return (y_prompt, y_sample, nck, ncv, nst)
```

```python
import numpy as np
from contextlib import ExitStack
import concourse.bass as bass
import concourse.mybir as mybir
from concourse.bass_utils import run_bass_kernel_spmd

F32 = mybir.dt.float32
BF16 = mybir.dt.bfloat16
AF = mybir.ActivationFunctionType
ALU = mybir.AluOpType

NL = 2
DM = 1024
NTOK = 1024
DIN = 4608
LAM = 0.606531
NORM_EPS = 1e-6
GN_EPS = 64e-5
NV = 384
SLOT = 4096

def i_ng(l, k, c): return l * 32 + k * 8 + c
def i_bmod(l, j): return 64 + l * 48 + j
def i_mu(l, i, c): return 160 + l * 12 + i * 4 + c
def i_kk(l, c): return 184 + l * 4 + c
def i_ka(l, c): return 192 + l * 4 + c
def i_rk(l, c): return 200 + l * 4 + c
def i_w0(l, d, c): return 208 + l * 8 + d * 4 + c
def i_a0(l, d, c): return 224 + l * 8 + d * 4 + c
def i_gnw(l, c): return 240 + l * 4 + c
def i_gnb(l, c): return 248 + l * 4 + c
def i_qg(l): return 256 + l
def i_kg(l): return 258 + l
I_CCTX = 260
I_CS = 268


class T:
    __slots__ = ("w", "r", "dsem", "dcnt")

    def __init__(self):
        self.w = None
        self.r = {}
        self.dsem = None
        self.dcnt = 0


class Sched:
    EPOCH = 4000

    def __init__(self, nc, es):
        self.nc = nc
        self.es = es
        self.eng = {"pe": nc.tensor, "dve": nc.vector, "act": nc.scalar, "pool": nc.gpsimd, "sp": nc.sync}
        self.cnt = {k: 0 for k in self.eng}
        self.sems = {k: [] for k in self.eng}
        self.seen = {k: {} for k in self.eng}
        self.nsem = 0
        self.nwait = 0

    def new_sem(self, name):
        self.nsem += 1
        return self.es.enter_context(self.nc.semaphore(f"{name}_{self.nsem}"))

    def _sem_for(self, e, idx):
        ep = (idx - 1) // self.EPOCH
        while len(self.sems[e]) <= ep:
            self.sems[e].append(self.new_sem(f"s_{e}"))
        return self.sems[e][ep], (idx - 1) % self.EPOCH + 1

    def _wait(self, on, dep):
        if dep is None:
            return
        if dep[0] == "dma":
            _, sem, val = dep
            key = ("dma", id(sem))
            if self.seen[on].get(key, 0) >= val:
                return
            self.eng[on].wait_ge(sem, val)
            self.nwait += 1
            self.seen[on][key] = val
        else:
            e, idx = dep
            if e == on and on == "pe":
                return
            if self.seen[on].get(e, 0) >= idx:
                return
            sem, val = self._sem_for(e, idx)
            self.eng[on].wait_ge(sem, val)
            self.nwait += 1
            self.seen[on][e] = idx

    def deps(self, on, reads, writes):
        for t in reads:
            self._wait(on, t.w)
        for t in writes:
            self._wait(on, t.w)
            for k, v in t.r.items():
                if isinstance(k, tuple):
                    self._wait(on, ("dma", k[1], v))
                else:
                    self._wait(on, (k, v))

    def op(self, on, reads, writes, fn):
        self.deps(on, reads, writes)
        ins = fn(self.eng[on])
        self.cnt[on] += 1
        n = self.cnt[on]
        sem, val = self._sem_for(on, n)
        ins.then_inc(sem, 1)
        for t in reads:
            t.r[on] = n
        for t in writes:
            t.w = (on, n)
            t.r = {}
        return ins

    def dma(self, on, out_ap, in_ap, reads, writes, **kw):
        self.deps(on, reads, writes)
        tgt = writes[0] if writes else reads[0]
        if tgt.dsem is None:
            tgt.dsem = self.new_sem("d")
        tgt.dcnt += 16
        ins = self.eng[on].dma_start(out=out_ap, in_=in_ap, **kw)
        ins.then_inc(tgt.dsem, 16)
        dep = ("dma", tgt.dsem, tgt.dcnt)
        for t in writes:
            t.w = dep
            t.r = {}
        for t in reads:
            t.r[("dma", tgt.dsem)] = tgt.dcnt
        return dep

    def barrier(self):
        es_ = ("pe", "dve", "act")
        for a in ("pe", "dve", "act", "pool", "sp"):
            for b in es_:
                if a != b and self.cnt[b]:
                    self._wait(a, (b, self.cnt[b]))

    def finish(self, out_deps):
        best = {}
        for d in out_deps:
            k = id(d[1])
            if k not in best or best[k][2] < d[2]:
                best[k] = d
        for d in best.values():
            self._wait("sp", d)
        for e in ("pe", "dve", "act"):
            if self.cnt[e]:
                self._wait("sp", (e, self.cnt[e]))


class Buf:
    _n = [0]

    def __init__(self, nc, es, name, shape, dt):
        Buf._n[0] += 1
        self.a = es.enter_context(nc.sbuf_tensor(f"{name}_{Buf._n[0]}", shape, dt))
        self.ts = {}

    def t(self, *key):
        if key not in self.ts:
            self.ts[key] = T()
        return self.ts[key]


def build(dbg=False, stop=None, units=(0, 1), layers=(0, 1)):
    nc = bass.Bass("TRN2", target_bir_lowering=False)
    dI = lambda n, s: nc.dram_tensor(n, s, F32, kind="ExternalInput").ap()
    dO = lambda n, s: nc.dram_tensor(n, s, F32, kind="ExternalOutput").ap()
    xin = [dI("xp", [NTOK, DM]), dI("xs", [NTOK, DM])]
    ck_d = dI("ck", [NL, 256, 128])
    cv_d = dI("cv", [NL, 256, 128])
    st_d = dI("st", [NL, 2, 8, 64, 64])
    vt_d = dI("vt", [NV, 128])
    cstf_d = dI("cstf", [128, 768])
    cstb_d = dI("cstb", [128, 1728 + 7 * 512 + 256])
    rope_d = dI("rope", [128, 2048])
    w_in = dI("w_in", [NL, DM, DIN]).rearrange("l (kc p) n -> l p kc n", p=128)
    w_br = dI("w_br", [NL, 2, 512, DM]).rearrange("l i (kc p) n -> l p (i kc) n", p=128)
    w_out = dI("w_out", [NL, DM, DM]).rearrange("l (kc p) n -> l p kc n", p=128)
    w_mod = dI("w_mod", [NL, DM, 6 * DM]).rearrange("l (kc p) n -> l p kc n", p=128)
    mlp_up = dI("mlp_up", [NL, DM, 4 * DM]).rearrange("l (kc p) n -> l p kc n", p=128)
    mlp_dn = dI("mlp_down", [NL, 4 * DM, DM]).rearrange("l (kc p) n -> l p kc n", p=128)
    dec_up = dI("decay_up", [NL, 2, 64, 512])
    icl_up = dI("iclr_up", [NL, 2, 64, 512])
    gate_up = dI("gate_up", [NL, 128, 512])
    yout = [dO("yp", [NTOK, DM]), dO("ys", [NTOK, DM])]
    nck_d = dO("nck", [4, NL, 256, 128])
    ncv_d = dO("ncv", [4, NL, 256, 128])
    nst_d = dO("nst", [4, NL, 2, 8, 64, 64])
    dumps = {}
    out_deps = []

    with ExitStack() as es:
        S = Sched(nc, es)
        mk = lambda name, shape, dt, st=es: Buf(nc, st, name, shape, dt)

        X = mk("X", [128, 8, NTOK], F32)
        H = mk("H", [128, 8, NTOK], BF16)
        SLOTS = [mk(f"slot{i}", [128, SLOT], BF16) for i in range(3)]
        CSTF = mk("CSTF", [128, 768], F32)
        CSTB = mk("CSTB", [128, 1728 + 7 * 512 + 256], BF16)
        VTT = mk("VTT", [128, NV], F32)
        DER = mk("DER", [128, 256], F32)
        MOD = mk("MOD", [128, NL, 2, 48], F32)
        DU = mk("DU", [128, NL, 2, 512], BF16)
        GU = mk("GU", [128, NL, 512], BF16)
        STG = [mk(f"STG{i}", [128, 1024], F32) for i in range(2)]
        SO = mk("SO", [64, 2, 128], F32)
        PSB = []
        for i in range(8):
            p = es.enter_context(nc.psum_tensor(f"ps{i}", [128, 512], F32))
            PSB.append((p, T()))

        IDENT = CSTF.a[:, 0:128]
        BLK = CSTF.a[:, 128:256]
        ONES = CSTF.a[:, 256:384]
        ROT = CSTF.a[:, 384:512]
        SEL = [CSTF.a[:, 512:640], CSTF.a[:, 640:768]]
        tCF = CSTF.t()
        MASKA = CSTB.a[:, 0:512]
        MASKB = CSTB.a[:, 512:1024]
        MASKC = CSTB.a[:, 1024:1536]
        IDB = CSTB.a[:, 1536:1664]
        ONESB = CSTB.a[:, 1664:1728]
        LMASK = [CSTB.a[:, 1728 + i * 512:1728 + (i + 1) * 512] for i in range(7)]
        ONES16 = CSTB.a[:, 5312:5440]
        BLK16 = CSTB.a[:, 5440:5568]
        tCB = CSTB.t()
        tV = VTT.t()
        tD = DER.t()
        tM = MOD.t()
        vcol = lambda i, n=1: VTT.a[:, i:i + n]
        D_OMM, D_HMU, D_OMKA, D_A1, D_G1, D_A2, D_G2 = 0, 24, 48, 56, 88, 120, 152
        dcol = lambda i, n=1: DER.a[:, i:i + n]
        d_lu = lambda base, l, u, c: base + (l * 2 + u) * 8 + c

        def MM(out, lhsT, rhs, R, W, start=True, stop=True):
            S.op("pe", R, W, lambda e: e.matmul(out, lhsT=lhsT, rhs=rhs, start=start, stop=stop))

        def ACT(out, in_, func, R, W, scale=1.0, bias=None):
            if bias is None:
                S.op("act", R, W, lambda e: e.activation(out=out, in_=in_, func=func, scale=scale))
            else:
                S.op("act", R, W, lambda e: e.activation(out=out, in_=in_, func=func, scale=scale, bias=bias))

        def TT(out, in0, in1, op, R, W, eng="dve"):
            S.op(eng, R, W, lambda e: e.tensor_tensor(out=out, in0=in0, in1=in1, op=op))

        def TS(out, in0, s1, s2, op0, op1, R, W, eng="dve"):
            if op1 is None:
                S.op(eng, R, W, lambda e: e.tensor_scalar(out=out, in0=in0, scalar1=s1, scalar2=None, op0=op0))
            else:
                S.op(eng, R, W, lambda e: e.tensor_scalar(out=out, in0=in0, scalar1=s1, scalar2=s2, op0=op0, op1=op1))

        def STT(out, in0, sc, in1, op0, op1, R, W, eng="dve"):
            S.op(eng, R, W, lambda e: e.scalar_tensor_tensor(out=out, in0=in0, scalar=sc, in1=in1, op0=op0, op1=op1))

        def CP(out, in_, R, W, eng="dve"):
            if eng == "act":
                ACT(out, in_, AF.Copy, R, W)
            else:
                S.op(eng, R, W, lambda e: e.tensor_copy(out=out, in_=in_))

        def RSQ(out, in_, scale, bias, R, W):
            S.op("act", R, W, lambda e: e.activation(out=out, in_=in_, func=AF.Ln, scale=scale, bias=bias))
            S.op("act", W, W, lambda e: e.activation(out=out, in_=out, func=AF.Exp, scale=-0.5))

        evc = [0]

        def EV(out, in_, R, W):
            evc[0] += 1
            CP(out, in_, R, W, eng=("act" if evc[0] % 2 else "dve"))

        psc = [0]

        def ps_rot(lo=0, hi=6):
            psc[0] += 1
            return PSB[lo + psc[0] % (hi - lo)]

        dcount = [0]

        def dump(name, ap, t, shape, dt=F32):
            if not dbg:
                return
            d = nc.dram_tensor("dbg_" + name, shape, dt, kind="ExternalOutput").ap()
            out_deps.append(S.dma("sp", d, ap, t if isinstance(t, list) else [t], []))
            dumps[name] = (shape, dt)

        class WS:
            def __init__(self):
                self.plan = []
                self.nxt = 0
                self.iss = 0
                self.busy = [False] * len(SLOTS)

            def add(self, tag, parts):
                self.plan.append((tag, parts))

            def _pump(self):
                while self.iss < len(self.plan) and not self.busy[self.iss % len(SLOTS)]:
                    tag, parts = self.plan[self.iss]
                    sl = SLOTS[self.iss % len(SLOTS)]
                    for (off, a, b, src) in parts:
                        dst = sl.a[:, off:off + a * b].rearrange("p (a b) -> p a b", b=b)
                        S.dma("pool", dst, src, [], [sl.t()])
                    self.busy[self.iss % len(SLOTS)] = True
                    self.iss += 1

            def pop(self, tag):
                i = self.nxt
                assert self.plan[i][0] == tag, (self.plan[i][0], tag)
                self._pump()
                assert self.iss > i, ("weight slot deadlock", tag)
                self.nxt += 1
                sl = SLOTS[i % len(SLOTS)]
                return sl.a, sl.t(), i

            def rel(self, i):
                self.busy[i % len(SLOTS)] = False
                self._pump()

        ws = WS()
        for u in units:
            for l in layers:
                ws.add(f"lw{u}{l}", [(0, 8, 256, w_in[l][:, :, 1536:1792])])
                for c in range(4):
                    ws.add(f"rkv{u}{l}{c}", [(i * 1024, 8, 128, w_in[l][:, :, i * 512 + c * 128:i * 512 + (c + 1) * 128]) for i in range(3)])
                ws.add(f"q{u}{l}", [(0, 8, 512, w_in[l][:, :, 1792:2304])])
                ws.add(f"kv{u}{l}", [(0, 8, 256, w_in[l][:, :, 2304:2560])])
                for jj in range(2):
                    ws.add(f"g{u}{l}{jj}", [(0, 8, 512, w_in[l][:, :, 2560 + jj * 512:2560 + (jj + 1) * 512])])
                    ws.add(f"g{u}{l}{2 + jj}", [(0, 8, 512, w_in[l][:, :, 3584 + jj * 512:3584 + (jj + 1) * 512])])
                for jt in range(2):
                    ws.add(f"wo{u}{l}{jt}", [(0, 8, 512, w_out[l][:, :, jt * 512:(jt + 1) * 512])])
                for tb in range(8):
                    ws.add(f"up{u}{l}{tb}", [(0, 8, 512, mlp_up[l][:, :, tb * 512:(tb + 1) * 512])])
                for j in range(8):
                    ws.add(f"dn{u}{l}{j}", [(0, 32, 128, mlp_dn[l][:, :, j * 128:(j + 1) * 128])])

        S.dma("sp", CSTF.a[:], cstf_d, [], [tCF])
        S.dma("pool", CSTB.a[:], cstb_d, [], [tCB])
        for l in range(NL):
            for d in range(2):
                S.dma("pool", DU.a[0:64, l, d, :], dec_up[l, d], [], [DU.t()])
                S.dma("pool", DU.a[64:128, l, d, :], icl_up[l, d], [], [DU.t()])
            S.dma("pool", GU.a[:, l, :], gate_up[l], [], [GU.t()])
        with ExitStack() as p0:
            VTS = mk("VTS", [128, 3, 128], F32, p0)
            WM = [mk(f"WM{i}", [128, 8, 512], BF16, p0) for i in range(3)]
            CSIL = mk("CSIL", [128, 16], BF16, p0)
            S.dma("sp", VTS.a[:], vt_d.rearrange("(a p) f -> p a f", p=128), [], [VTS.t()])
            pv, tpv = PSB[0]
            for a in range(3):
                MM(pv[:, a * 128:(a + 1) * 128], VTS.a[:, a, :], IDENT, [VTS.t(), tCF], [tpv])
            CP(VTT.a[:, 0:384], pv[:, 0:384], [tpv], [tV])
            ACT(CSIL.a[:], vcol(I_CCTX, 16), AF.Silu, [tV], [CSIL.t()])
            csv = CSIL.a[:].rearrange("p (u k) -> p u k", k=8)
            TS(dcol(D_OMM, 24), vcol(i_mu(0, 0, 0), 24), -1.0, 1.0, ALU.mult, ALU.add, [tV], [tD])
            TS(dcol(D_HMU, 24), vcol(i_mu(0, 0, 0), 24), 0.5, None, ALU.mult, None, [tV], [tD])
            TS(dcol(D_OMKA, 8), vcol(i_ka(0, 0), 8), -1.0, 1.0, ALU.mult, ALU.add, [tV], [tD])
            wmi = 0
            for l in range(NL):
                for blk in range(12):
                    wm = WM[wmi % 3]
                    wmi += 1
                    S.dma("pool", wm.a[:], w_mod[l][:, :, blk * 512:(blk + 1) * 512], [], [wm.t()])
                    pm, tpm = PSB[1 + blk % 2]
                    for jj in range(4):
                        for kc in range(8):
                            MM(pm[:, jj * 2:jj * 2 + 2], wm.a[:, kc, jj * 128:(jj + 1) * 128], csv[:, :, kc],
                               [wm.t(), CSIL.t()], [tpm], start=(kc == 0), stop=(kc == 7))
                    pmv = pm[:, 0:8].rearrange("p (j u) -> p j u", u=2)
                    for u in range(2):
                        TT(MOD.a[:, l, u, blk * 4:blk * 4 + 4], pmv[:, :, u], vcol(i_bmod(l, blk * 4), 4), ALU.add, [tpm, tV], [tM])
                for u in range(2):
                    STT(dcol(d_lu(D_A1, l, u, 0), 8), MOD.a[:, l, u, 8:16], 1.0, vcol(i_ng(l, 0, 0), 8), ALU.add, ALU.mult, [tM, tV], [tD])
                    TT(dcol(d_lu(D_G1, l, u, 0), 8), MOD.a[:, l, u, 16:24], vcol(i_ng(l, 1, 0), 8), ALU.mult, [tM, tV], [tD])
                    STT(dcol(d_lu(D_A2, l, u, 0), 8), MOD.a[:, l, u, 32:40], 1.0, vcol(i_ng(l, 2, 0), 8), ALU.add, ALU.mult, [tM, tV], [tD])
                    TT(dcol(d_lu(D_G2, l, u, 0), 8), MOD.a[:, l, u, 40:48], vcol(i_ng(l, 3, 0), 8), ALU.mult, [tM, tV], [tD])
            dump("mod", MOD.a[:].rearrange("p l u j -> p (l u j)"), tM, [128, NL * 2 * 48])
            S.barrier()
        if stop == "p0":
            S.finish(out_deps)
            return nc, dumps

        HS = [slice(0, 512), slice(512, 1024)]

        def rstd_from(src_fn, nch, lhsT, scale, eps, RSTD, tR, SQ):
            pss, tps = PSB[7]
            for c in range(nch):
                sq = SQ[c % 2]
                ap, ts_ = src_fn(c)
                ACT(sq.a[:], ap, AF.Square, ts_, [sq.t()])
                MM(pss[:], lhsT, sq.a[:], [sq.t(), tCB], [tps], start=(c == 0), stop=(c == nch - 1))
            RSQ(RSTD, pss[:], scale, eps, [tps], [tR])

        def norm_mod(u, l, A_base, sh_off, pst):
            SQ = [mk(f"nSQ{i}", [128, 512], BF16, pst) for i in range(2)]
            RS = mk("nRS", [128, 512], F32, pst)
            TMP = [mk(f"nTMP{i}", [128, 512], F32, pst) for i in range(2)]
            for h in range(2):
                rstd_from(lambda c: (X.a[:, c, HS[h]], [X.t(c, h)]), 8, ONES16, 1.0 / DM, NORM_EPS, RS.a[:], RS.t(), SQ)
                for c in range(8):
                    tmp = TMP[c % 2]
                    TT(tmp.a[:], X.a[:, c, HS[h]], RS.a[:], ALU.mult, [X.t(c, h), RS.t()], [tmp.t()])
                    ACT(H.a[:, c, HS[h]], tmp.a[:], AF.Identity, [tmp.t(), tD, tM], [H.t(c, h)],
                        scale=dcol(d_lu(A_base, l, u, c)), bias=MOD.a[:, l, u, sh_off + c:sh_off + c + 1])

        def resid_add(u, l, SRC, G_base, h, pst_bufs):
            SQ, RS, TMP = pst_bufs
            rstd_from(lambda c: (SRC.a[:, c, :], [SRC.t(c)]), 8, ONES16, 1.0 / DM, NORM_EPS, RS.a[:], RS.t(), SQ)
            for c in range(8):
                tmp = TMP[c % 2]
                TT(tmp.a[:], SRC.a[:, c, :], RS.a[:], ALU.mult, [SRC.t(c), RS.t()], [tmp.t()])
                STT(X.a[:, c, HS[h]], tmp.a[:], dcol(d_lu(G_base, l, u, c)), X.a[:, c, HS[h]], ALU.mult, ALU.add,
                    [tmp.t(), tD, X.t(c, h)], [X.t(c, h)])

        def load_x(u):
            for tt in range(8):
                stg = STG[tt % 2]
                S.dma("sp", stg.a[:], xin[u][tt * 128:(tt + 1) * 128, :], [], [stg.t()])
                for g in range(2):
                    pb, tpb = ps_rot()
                    for cc in range(4):
                        c = g * 4 + cc
                        MM(pb[:, cc * 128:(cc + 1) * 128], stg.a[:, c * 128:(c + 1) * 128], IDENT, [stg.t(), tCF], [tpb])
                    EV(X.a[:, g * 4:(g + 1) * 4, tt * 128:(tt + 1) * 128], pb[:].rearrange("p (c t) -> p c t", t=128),
                       [tpb], [X.t(c_, tt // 4) for c_ in range(g * 4, g * 4 + 4)])

        def store_y(u):
            for tt in range(8):
                stg = STG[tt % 2]
                for g in range(2):
                    pb, tpb = ps_rot()
                    for cc in range(4):
                        c = g * 4 + cc
                        MM(pb[:, cc * 128:(cc + 1) * 128], X.a[:, c, tt * 128:(tt + 1) * 128], IDENT, [X.t(c, tt // 4), tCF], [tpb])
                    EV(stg.a[:, g * 512:(g + 1) * 512], pb[:], [tpb], [stg.t()])
                out_deps.append(S.dma("sp", yout[u][tt * 128:(tt + 1) * 128, :], stg.a[:], [stg.t()], []))

        def layer(u, l):
            B = 4 if u == 0 else 1
            TL = NTOK // B
            nT = TL // 128
            with ExitStack() as pst:
                norm_mod(u, l, D_A1, 0, pst)
                S.barrier()
            if dbg and stop == "n1":
                dump("h", H.a[:].rearrange("p c t -> p (c t)"), list(H.ts.values()), [128, 8 * NTOK], BF16)
                return True
            with ExitStack() as mix:
                OR_ = mk("OR", [128, 4, NTOK], BF16, mix)
                LWLA = mk("LWLA", [128, NTOK], BF16, mix)
                SLG = mk("SLG", [128, NTOK], BF16, mix)
                wsl, twsl, wi = ws.pop(f"lw{u}{l}")
                wv = wsl[:, 0:2048].rearrange("p (a b) -> p a b", b=256)
                for h in range(2):
                    for ch in range(2):
                        pb, tpb = ps_rot()
                        for kc in range(8):
                            MM(pb[:], wv[:, kc, ch * 128:(ch + 1) * 128], H.a[:, kc, HS[h]], [twsl, H.t(kc, h)], [tpb], start=(kc == 0), stop=(kc == 7))
                        if ch == 0:
                            ACT(LWLA.a[0:64, HS[h]], pb[0:64, :], AF.Tanh, [tpb], [LWLA.t(h)])
                            CP(LWLA.a[64:128, HS[h]], pb[64:128, :], [tpb], [LWLA.t(h)])
                        else:
                            ACT(SLG.a[:, HS[h]], pb[:], AF.Sigmoid, [tpb], [SLG.t(h)])
                ws.rel(wi)
                if dbg and stop == "lw":
                    dump("lwla", LWLA.a[:], list(LWLA.ts.values()), [128, NTOK], BF16)
                    dump("slg", SLG.a[:], list(SLG.ts.values()), [128, NTOK], BF16)
                    return True
                with ExitStack() as rw:
                    r = rwkv(u, l, B, TL, nT, OR_, LWLA, SLG, rw)
                    S.barrier()
                if r:
                    return True
                OA_ = mk("OA", [128, 4, NTOK], BF16, mix)
                with ExitStack() as at:
                    r = attention(u, l, B, TL, OA_, at)
                    S.barrier()
                if r:
                    return True
                with ExitStack() as mg:
                    r = merge(u, l, OR_, OA_, mg)
                    S.barrier()
                if r:
                    return True
            with ExitStack() as pst:
                norm_mod(u, l, D_A2, 24, pst)
                S.barrier()
            with ExitStack() as ml:
                r = mlp(u, l, ml)
                S.barrier()
            return r

        def rwkv(u, l, B, TL, nT, OR_, LWLA, SLG, rw):
            f32b = lambda n, st: mk(n, [128, NTOK], F32, st)
            b16b = lambda n, st: mk(n, [128, NTOK], BF16, st)
            BON = f32b("BON", rw)
            GATE = b16b("GATE", rw)
            YACC = f32b("YACC", rw)
            RESET = f32b("RESET", rw)
            RT = [b16b(f"RTl{d}", rw) for d in range(2)]
            KTT = [b16b(f"KTT{d}", rw) for d in range(2)]
            BN = [b16b(f"BN{d}", rw) for d in range(2)]
            KHT = [b16b(f"KHT{d}", rw) for d in range(2)]
            VB = b16b("VB", rw)
            PC = mk("PC", [128, 2, 8], F32, rw)
            TOT = mk("TOT", [128, 8], F32, rw)
            G32 = mk("G32", [128, 4, 128], F32, rw)
            GB = mk("GB", [128, 4, 128], BF16, rw)
            GT = mk("GT", [128, 128], F32, rw)
            ST = mk("ST", [128, 2, 128], F32, rw)
            if u == 1:
                S.op("dve", [], [ST.t()], lambda e: e.memset(ST.a[:], 0.0))

            S.op("dve", [], [RESET.t()], lambda e: e.memset(RESET.a[:], 1.0))
            S.op("dve", [RESET.t()], [RESET.t()], lambda e: e.memset(RESET.a[:].rearrange("p (a b) -> p a b", b=128)[:, :, 0:1], 0.0))
            tile_ = lambda buf, tt: buf.a[:, tt * 128:(tt + 1) * 128]

            def prepA(c, pp):
                RKV = [f32b(f"RKV{i}", pp) for i in range(3)]
                TMP = [f32b(f"RT{i}", pp) for i in range(4)]
                KH = f32b("KH", pp)
                KTS = f32b("KTS", pp)
                wsl, twsl, wi = ws.pop(f"rkv{u}{l}{c}")
                wv = wsl[:, 0:3072].rearrange("p (i a b) -> p i a b", a=8, b=128)
                for i in range(3):
                    for h in range(2):
                        pb, tpb = ps_rot()
                        for kc in range(8):
                            MM(pb[:], wv[:, i, kc, :], H.a[:, kc, HS[h]], [twsl, H.t(kc, h)], [tpb], start=(kc == 0), stop=(kc == 7))
                        EV(RKV[i].a[:, HS[h]], pb[:], [tpb], [RKV[i].t(h)])
                ws.rel(wi)
                allh = lambda bf: [bf.t(0), bf.t(1)]
                for i in range(3):
                    Z = RKV[i]
                    zv = Z.a[:].rearrange("p (b t) -> p b t", t=TL)
                    tv = TMP[0].a[:].rearrange("p (b t) -> p b t", t=TL)
                    TT(tv[:, :, 1:TL - 1], zv[:, :, 0:TL - 2], zv[:, :, 2:TL], ALU.add, allh(Z), allh(TMP[0]))
                    CP(tv[:, :, 0:1], zv[:, :, 1:2], allh(Z), allh(TMP[0]))
                    CP(tv[:, :, TL - 1:TL], zv[:, :, TL - 2:TL - 1], allh(Z), allh(TMP[0]))
                    for h in range(2):
                        ACT(Z.a[:, HS[h]], Z.a[:, HS[h]], AF.Identity, [Z.t(h), tD, TMP[0].t(h)], [Z.t(h)], scale=dcol(D_OMM + l * 12 + i * 4 + c))
                    for h in range(2):
                        STT(Z.a[:, HS[h]], TMP[0].a[:, HS[h]], dcol(D_HMU + l * 12 + i * 4 + c), Z.a[:, HS[h]], ALU.mult, ALU.add,
                            [TMP[0].t(h), tD, Z.t(h)], [Z.t(h)])
                return RKV, TMP, KH, KTS

            def prepB(c, RKV, TMP, KH, KTS):
                allh = lambda bf: [bf.t(0), bf.t(1)]
                R_, K_, V_ = RKV
                H2 = range(2)
                for h in H2:
                    CP(VB.a[:, HS[h]], V_.a[:, HS[h]], [V_.t(h)], [VB.t(h)], eng="act")
                for h in H2:
                    ACT(TMP[0].a[:, HS[h]], K_.a[:, HS[h]], AF.Identity, [K_.t(h), tV], [TMP[0].t(h)], scale=vcol(i_kk(l, c)))
                for h in H2:
                    ACT(TMP[1].a[:, HS[h]], TMP[0].a[:, HS[h]], AF.Square, [TMP[0].t(h)], [TMP[1].t(h)])
                for h in H2:
                    pb, tpb = ps_rot()
                    MM(pb[:], BLK, TMP[1].a[:, HS[h]], [tCF, TMP[1].t(h)], [tpb])
                    RSQ(TMP[2].a[:, HS[h]], pb[:], 1.0, 1e-12, [tpb], [TMP[2].t(h)])
                for h in H2:
                    TT(KH.a[:, HS[h]], TMP[0].a[:, HS[h]], TMP[2].a[:, HS[h]], ALU.mult, [TMP[0].t(h), TMP[2].t(h)], [KH.t(h)])
                for d in range(2):
                    SIG, CUM, EX, AA = TMP
                    hs = lambda bf, h: bf.a[:, HS[h]]
                    for h in H2:
                        pb, tpb = ps_rot()
                        MM(pb[:], DU.a[0:64, l, d, c * 128:(c + 1) * 128], LWLA.a[0:64, HS[h]], [DU.t(), LWLA.t(h)], [tpb])
                        ACT(hs(SIG, h), pb[:], AF.Sigmoid, [tpb, tV], [SIG.t(h)], bias=vcol(i_w0(l, d, c)))
                        pb, tpb = ps_rot()
                        MM(pb[:], DU.a[64:128, l, d, c * 128:(c + 1) * 128], LWLA.a[64:128, HS[h]], [DU.t(), LWLA.t(h)], [tpb])
                        ACT(hs(AA, h), pb[:], AF.Sigmoid, [tpb, tV], [AA.t(h)], bias=vcol(i_a0(l, d, c)))
                    for h in H2:
                        S.op("dve", [RESET.t(), SIG.t(h)], [CUM.t(h)], lambda e: e.tensor_tensor_scan(
                            out=hs(CUM, h), data0=RESET.a[:, HS[h]], data1=hs(SIG, h), initial=0.0, op0=ALU.mult, op1=ALU.add))
                    cv3 = lambda h: hs(CUM, h).rearrange("p (a b) -> p a b", b=128)
                    toth = lambda h: TOT.a[:, 4 * h:4 * h + 4]
                    for h in H2:
                        CP(toth(h).unsqueeze(2), cv3(h)[:, :, 127:128], [CUM.t(h)], [TOT.t(h)])
                    for h in H2:
                        ACT(PC.a[:, d, 4 * h:4 * h + 4], toth(h), AF.Exp, [TOT.t(h)], [PC.t(h)], scale=-LAM)
                    for h in H2:
                        if d == 0:
                            TT(hs(EX, h), hs(CUM, h), hs(SIG, h), ALU.subtract, [CUM.t(h), SIG.t(h)], [EX.t(h)])
                        else:
                            TT(hs(EX, h).rearrange("p (a b) -> p a b", b=128), toth(h).unsqueeze(2).to_broadcast([128, 4, 128]), cv3(h),
                               ALU.subtract, [TOT.t(h), CUM.t(h)], [EX.t(h)])
                    if d == 1:
                        for h in H2:
                            TT(hs(CUM, h), hs(EX, h), hs(SIG, h), ALU.add, [EX.t(h), SIG.t(h)], [CUM.t(h)])
                    for h in H2:
                        ACT(hs(EX, h), hs(EX, h), AF.Exp, [EX.t(h)], [EX.t(h)], scale=-LAM)
                    for h in H2:
                        TT(hs(KHT[d], h), hs(KH, h), hs(EX, h), ALU.mult, [KH.t(h), EX.t(h)], [KHT[d].t(h)])
                    for h in H2:
                        ACT(hs(EX, h), hs(CUM, h), AF.Exp, [CUM.t(h)], [EX.t(h)], scale=-LAM)
                    for h in H2:
                        TT(hs(RT[d], h), hs(R_, h), hs(EX, h), ALU.mult, [R_.t(h), EX.t(h)], [RT[d].t(h)])
                    for h in H2:
                        ACT(hs(CUM, h), hs(CUM, h), AF.Exp, [CUM.t(h)], [CUM.t(h)], scale=LAM)
                    for h in H2:
                        ACT(hs(SIG, h), hs(AA, h), AF.Identity, [AA.t(h), tV, tD], [SIG.t(h)], scale=vcol(i_ka(l, c)), bias=dcol(D_OMKA + l * 4 + c))
                    for h in H2:
                        TT(hs(SIG, h), hs(SIG, h), hs(K_, h), ALU.mult, [SIG.t(h), K_.t(h)], [SIG.t(h)])
                    for h in H2:
                        if d == 0:
                            CP(hs(KTS, h), hs(SIG, h), [SIG.t(h)], [KTS.t(h)])
                        else:
                            TT(hs(KTS, h), hs(KTS, h), hs(SIG, h), ALU.add, [SIG.t(h), KTS.t(h)], [KTS.t(h)])
                    for h in H2:
                        TT(hs(KTT[d], h), hs(SIG, h), hs(CUM, h), ALU.mult, [SIG.t(h), CUM.t(h)], [KTT[d].t(h)])
                    for h in H2:
                        TT(hs(AA, h), hs(AA, h), hs(KH, h), ALU.mult, [AA.t(h), KH.t(h)], [AA.t(h)])
                    for h in H2:
                        STT(hs(BN[d], h), hs(AA, h), -1.0, hs(CUM, h), ALU.mult, ALU.mult, [AA.t(h), CUM.t(h)], [BN[d].t(h)])
                for h in H2:
                    STT(TMP[0].a[:, HS[h]], KTS.a[:, HS[h]], vcol(i_rk(l, c)), R_.a[:, HS[h]], ALU.mult, ALU.mult, [KTS.t(h), tV, R_.t(h)], [TMP[0].t(h)])
                for h in H2:
                    pb, tpb = ps_rot()
                    MM(pb[:], BLK, TMP[0].a[:, HS[h]], [tCF, TMP[0].t(h)], [tpb])
                    TT(BON.a[:, HS[h]], pb[:], V_.a[:, HS[h]], ALU.mult, [tpb, V_.t(h)], [BON.t(h)])
                for h in H2:
                    pb, tpb = ps_rot()
                    MM(pb[:], GU.a[:, l, c * 128:(c + 1) * 128], SLG.a[:, HS[h]], [GU.t(), SLG.t(h)], [tpb])
                    EV(GATE.a[:, HS[h]], pb[:], [tpb], [GATE.t(h)])
                if dbg and stop == f"prep{c}":
                    for nm, bf in (("r", R_), ("k", K_), ("v", V_), ("kh", KH), ("bon", BON)):
                        dump(nm, bf.a[:], allh(bf), [128, NTOK])
                    for d in range(2):
                        for nm, bf in (("rt", RT), ("ktt", KTT), ("bn", BN), ("kht", KHT)):
                            dump(f"{nm}{d}", bf[d].a[:], allh(bf[d]), [128, NTOK], BF16)
                    dump("pc", PC.a[:].rearrange("p d t -> p (d t)"), allh(PC), [128, 16])
                    dump("gate", GATE.a[:], allh(GATE), [128, NTOK], BF16)
                    return True
                return False

            def groups_(c):
                if u == 0:
                    S.op("dve", [], [G32.t()], lambda e: e.memset(G32.a[:], 0.0))
                    S.op("dve", [], [GB.t()], lambda e: e.memset(GB.a[:], 0.0))
                else:
                    for d in range(2):
                        for hh in range(2):
                            S.dma("sp", ST.a[0:64, d, hh * 64:(hh + 1) * 64], st_d[l, d, 2 * c + hh, :, :], [], [ST.t()])
                    pb, tpb = ps_rot()
                    for d in range(2):
                        MM(pb[:, d * 64:(d + 1) * 64], ST.a[:, d, :], IDENT[:, 0:64], [ST.t(), tCF], [tpb])
                    CP(G32.a[:, 0, :], pb[:, 0:128], [tpb], [G32.t()])
                    CP(GB.a[:, 0, :], G32.a[:, 0, :], [G32.t()], [GB.t()])
                if dbg and stop == "st":
                    dump("g32", G32.a[:, 0, :], G32.t(), [128, 128])
                    return True
                yinit = set()
                with ExitStack() as gg:
                    NSETS, KPRE = 5, 3
                    CS = [dict(TM=mk("TM", [128, 8, 128], BF16, gg), AbT=mk("G_AbT", [128, 512], BF16, gg), AkT=mk("G_AkT", [128, 512], BF16, gg),
                               WkT=mk("G_WkT", [128, 512], BF16, gg), KHP=mk("KHP", [128, 256], BF16, gg), UU=mk("UU", [128, 256], BF16, gg)) for _ in range(NSETS)]
                    PS_ = [dict(N=mk("G_N", [128, 512], BF16, gg), Lk=mk("G_Lk", [128, 512], BF16, gg), Z=mk("G_Z", [128, 512], BF16, gg),
                                FT=mk("G_FT", [128, 512], BF16, gg), FF=[mk(f"FF{i}", [128, 512], BF16, gg) for i in range(2)]) for _ in range(KPRE)]
                    cset = {}
                    groups = [(b, i) for b in range(B) for i in range(nT)]
                    TIs = {}

                    def tl_of(b, i):
                        return [b * nT + i, b * nT + nT - 1 - i]

                    def bank_split_gram(LB, RB, tl, add_ident=False):
                        pbs = [ps_rot(), ps_rot()]
                        for hh in range(2):
                            pr = slice(hh * 64, (hh + 1) * 64)
                            for d in range(2):
                                o = pbs[hh][0][:, d * 128:(d + 1) * 128]
                                MM(o, LB[d].a[pr, tl[d] * 128:(tl[d] + 1) * 128], RB[d].a[pr, tl[d] * 128:(tl[d] + 1) * 128],
                                   [LB[d].t(tl[d] // 4), RB[d].t(tl[d] // 4)], [pbs[hh][1]], start=True, stop=not add_ident)
                                if add_ident:
                                    MM(o, IDB, IDB, [tCB], [pbs[hh][1]], start=False, stop=True)
                        return pbs

                    def pre_gen(g, pslot):
                        b, i = groups[g]
                        tl = tl_of(b, i)
                        C_ = CS[cset[g]]
                        P_ = PS_[pslot]
                        TM = C_["TM"]
                        srcs = [(KHT[0], tl[0]), (KTT[0], tl[0]), (BN[0], tl[0]), (KHT[1], tl[1]), (KTT[1], tl[1]), (BN[1], tl[1]), (VB, tl[0]), (VB, tl[1])]
                        for g2 in range(2):
                            pb, tpb = ps_rot()
                            for k in range(4):
                                bf, tt = srcs[g2 * 4 + k]
                                MM(pb[:, k * 128:(k + 1) * 128], tile_(bf, tt), IDB, [bf.t(tt // 4), tCB], [tpb])
                            EV(TM.a[:, g2 * 4:(g2 + 1) * 4, :], pb[:].rearrange("p (k f) -> p k f", f=128), [tpb], [TM.t()])
                        yield
                        pbs = bank_split_gram(KHT, BN, tl)
                        gv = P_["N"].a[:].rearrange("p (d h t) -> p d h t", d=2, h=2)
                        for hh in range(2):
                            CP(gv[:, :, hh, :], pbs[hh][0][:, 0:256].rearrange("p (d t) -> p d t", d=2), [pbs[hh][1]], [P_["N"].t()], eng="act")
                        yield
                        for nm, LB, RB, MSK, dstb, addi in (("NT", BN, KHT, LMASK[0], P_["FF"][0], False), ("Lk", KHT, KTT, MASKA, P_["Lk"], False),
                                                          ("AbT", BN, RT, MASKC, C_["AbT"], False), ("AkT", KTT, RT, MASKC, C_["AkT"], False)):
                            pbs = bank_split_gram(LB, RB, tl, add_ident=addi)
                            gv = dstb.a[:].rearrange("p (d h t) -> p d h t", d=2, h=2)
                            mv = MSK.rearrange("p (d h t) -> p d h t", d=2, h=2)
                            for hh in range(2):
                                TT(gv[:, :, hh, :], pbs[hh][0][:, 0:256].rearrange("p (d t) -> p d t", d=2), mv[:, :, hh, :], ALU.mult,
                                   [pbs[hh][1], tCB], [dstb.t()])
                            yield
                        fi = 0
                        FF = P_["FF"]
                        for lv in range(1, 7):
                            Fc, Fn = FF[fi], FF[1 - fi]
                            first = (lv == 1)
                            pb, tpb = ps_rot()
                            for q in range(4):
                                qs = slice(q * 128, (q + 1) * 128)
                                MM(pb[:, qs], Fc.a[:, qs], IDB, [Fc.t(), tCB], [tpb], start=True, stop=not first)
                                if first:
                                    MM(pb[:, qs], IDB, IDB, [tCB], [tpb], start=False, stop=True)
                            CP(P_["FT"].a[:], pb[:], [tpb], [P_["FT"].t()], eng="act")
                            pb, tpb = ps_rot()
                            for q in range(4):
                                qs = slice(q * 128, (q + 1) * 128)
                                MM(pb[:, qs], P_["N"].a[:, qs], Fc.a[:, qs], [P_["N"].t(), Fc.t()], [tpb], start=True, stop=not first)
                                if first:
                                    MM(pb[:, qs], P_["N"].a[:, qs], IDB, [P_["N"].t(), tCB], [tpb], start=False, stop=True)
                            TT(P_["Z"].a[:], pb[:], LMASK[lv], ALU.mult, [tpb, tCB], [P_["Z"].t()])
                            yield
                            on_dve = lv in (2, 5)
                            pb, tpb = ps_rot()
                            for q in range(4):
                                qs = slice(q * 128, (q + 1) * 128)
                                MM(pb[:, qs], P_["FT"].a[:, qs], P_["Z"].a[:, qs], [P_["FT"].t(), P_["Z"].t()], [tpb], start=True, stop=on_dve)
                                if not on_dve:
                                    MM(pb[:, qs], IDB, Fc.a[:, qs], [tCB, Fc.t()], [tpb], start=False, stop=not first)
                                    if first:
                                        MM(pb[:, qs], IDB, IDB, [tCB], [tpb], start=False, stop=True)
                            if on_dve:
                                TT(Fn.a[:], pb[:], Fc.a[:], ALU.add, [tpb, Fc.t()], [Fn.t()])
                            else:
                                CP(Fn.a[:], pb[:], [tpb], [Fn.t()], eng="act")
                            fi = 1 - fi
                            yield
                        TI = FF[fi]
                        tm = lambda k, hh: TM.a[:, k, hh * 64:(hh + 1) * 64]
                        pb, tpb = ps_rot()
                        for hh in range(2):
                            for d in range(2):
                                q = d * 2 + hh
                                MM(pb[hh * 64:(hh + 1) * 64, d * 128:(d + 1) * 128], tm(d * 3, hh), TI.a[:, q * 128:(q + 1) * 128], [TM.t(), TI.t()], [tpb])
                        CP(C_["KHP"].a[:], pb[:, 0:256], [tpb], [C_["KHP"].t()], eng="act")
                        pb, tpb = ps_rot()
                        for q in range(4):
                            qs = slice(q * 128, (q + 1) * 128)
                            MM(pb[:, qs], P_["Lk"].a[:, qs], TI.a[:, qs], [P_["Lk"].t(), TI.t()], [tpb])
                        CP(C_["WkT"].a[:], pb[:], [tpb], [C_["WkT"].t()])
                        yield

                    def chain_gen(g):
                        b, i = groups[g]
                        tl = tl_of(b, i)
                        C_ = CS[cset[g]]
                        TM, KHP, UU = C_["TM"], C_["KHP"], C_["UU"]
                        tm = lambda k, hh: TM.a[:, k, hh * 64:(hh + 1) * 64]
                        gb = lambda d, hh: GB.a[hh * 64:(hh + 1) * 64, b, d * 64:(d + 1) * 64]
                        pu, tpu = PSB[6]
                        for hh in range(2):
                            for d in range(2):
                                q = d * 2 + hh
                                MM(pu[:, q * 64:(q + 1) * 64], KHP.a[hh * 64:(hh + 1) * 64, d * 128:(d + 1) * 128], gb(d, hh), [KHP.t(), GB.t()], [tpu], start=True, stop=False)
                                MM(pu[:, q * 64:(q + 1) * 64], C_["WkT"].a[:, q * 128:(q + 1) * 128], tm(6 + d, hh), [C_["WkT"].t(), TM.t()], [tpu], start=False, stop=True)
                        CP(UU.a[:], pu[:, 0:256], [tpu], [UU.t()], eng="act")
                        yield
                        py, tpy = PSB[7]
                        for hh in range(2):
                            for d in range(2):
                                q = d * 2 + hh
                                pr = slice(hh * 64, (hh + 1) * 64)
                                o = py[pr, d * 128:(d + 1) * 128]
                                MM(o, gb(d, hh), RT[d].a[pr, tl[d] * 128:(tl[d] + 1) * 128], [GB.t(), RT[d].t(tl[d] // 4)], [tpy], start=True, stop=False)
                                MM(o, UU.a[:, q * 64:(q + 1) * 64], C_["AbT"].a[:, q * 128:(q + 1) * 128], [UU.t(), C_["AbT"].t()], [tpy], start=False, stop=False)
                                MM(o, tm(6 + d, hh), C_["AkT"].a[:, q * 128:(q + 1) * 128], [TM.t(), C_["AkT"].t()], [tpy], start=False, stop=True)
                        for d in range(2):
                            dst = tile_(YACC, tl[d])
                            if tl[d] in yinit:
                                TT(dst, py[:, d * 128:(d + 1) * 128], dst, ALU.add, [tpy, YACC.t(tl[d] // 4)], [YACC.t(tl[d] // 4)])
                            else:
                                yinit.add(tl[d])
                                CP(dst, py[:, d * 128:(d + 1) * 128], [tpy], [YACC.t(tl[d] // 4)], eng="act")
                        if not (u == 1 and i == nT - 1):
                            pg, tpg = PSB[6]
                            for hh in range(2):
                                for d in range(2):
                                    q = d * 2 + hh
                                    o = pg[hh * 64:(hh + 1) * 64, 256 + d * 64:256 + (d + 1) * 64]
                                    MM(o, tm(d * 3 + 1, hh), tm(6 + d, hh), [TM.t()], [tpg], start=True, stop=False)
                                    MM(o, tm(d * 3 + 2, hh), UU.a[:, q * 64:(q + 1) * 64], [TM.t(), UU.t()], [tpg], start=False, stop=True)
                            TT(GT.a[:], pg[:, 256:384], G32.a[:, b, :], ALU.add, [tpg, G32.t()], [GT.t()])
                            for d in range(2):
                                TS(G32.a[:, b, d * 64:(d + 1) * 64], GT.a[:, d * 64:(d + 1) * 64], PC.a[:, d, tl[d]:tl[d] + 1], None, ALU.mult, None, [GT.t(), PC.t(tl[d] // 4)], [G32.t()])
                            CP(GB.a[:, b, :], G32.a[:, b, :], [G32.t()], [GB.t()], eng="act")
                        if u == 0 and i == nT - 1:
                            pb, tpb = ps_rot()
                            for d in range(2):
                                MM(pb[0:64, d * 128:(d + 1) * 128], G32.a[:, b, d * 64:(d + 1) * 64], IDENT, [G32.t(), tCF], [tpb])
                            CP(SO.a[:].rearrange("v d k -> v (d k)"), pb[0:64, 0:256], [tpb], [SO.t()])
                            for d in range(2):
                                out_deps.append(S.dma("sp", nst_d[b, l, d, 2 * c:2 * c + 2, :, :].rearrange("h v k -> v h k"),
                                                      SO.a[:, d, :].rearrange("v (h k) -> v h k", k=64), [SO.t()], []))
                        yield

                    ng = len(groups)
                    free_sets = list(range(NSETS))
                    free_pslots = list(range(KPRE))
                    active = []
                    ready = set()
                    nxt_pre = 0
                    nxt_chain = 0
                    cgen = None
                    rnd = 0
                    last_start = -100
                    while nxt_chain < ng:
                        rnd += 1
                        if nxt_pre < ng and free_sets and free_pslots and rnd - last_start >= 1:
                            cset[nxt_pre] = free_sets.pop(0)
                            ps_ = free_pslots.pop(0)
                            active.append([pre_gen(nxt_pre, ps_), nxt_pre, ps_])
                            nxt_pre += 1
                            last_start = rnd
                        for ent in list(active):
                            try:
                                next(ent[0])
                            except StopIteration:
                                active.remove(ent)
                                ready.add(ent[1])
                                free_pslots.append(ent[2])
                        if cgen is None and nxt_chain in ready:
                            cgen = chain_gen(nxt_chain)
                        if cgen is not None:
                            try:
                                next(cgen)
                            except StopIteration:
                                free_sets.append(cset[nxt_chain])
                                cgen = None
                                nxt_chain += 1
                    S.barrier()
                return False

            def output_(c, oo):
                MU = f32b("MU", oo)
                T1 = f32b("T1", oo)
                T2 = f32b("T2", oo)
                H2 = range(2)
                hs = lambda bf, h: bf.a[:, HS[h]]
                for h in H2:
                    pb, tpb = ps_rot()
                    MM(pb[:], BLK, hs(YACC, h), [tCF, YACC.t(h)], [tpb])
                    STT(hs(MU, h), pb[:], -1.0 / 64, hs(YACC, h), ALU.mult, ALU.add, [tpb, YACC.t(h)], [MU.t(h)])
                for h in H2:
                    ACT(hs(T1, h), hs(MU, h), AF.Square, [MU.t(h)], [T1.t(h)])
                for h in H2:
                    pb, tpb = ps_rot()
                    MM(pb[:], BLK, hs(T1, h), [tCF, T1.t(h)], [tpb])
                    RSQ(hs(T2, h), pb[:], 1.0 / 64, GN_EPS, [tpb], [T2.t(h)])
                for h in H2:
                    TT(hs(MU, h), hs(MU, h), hs(T2, h), ALU.mult, [MU.t(h), T2.t(h)], [MU.t(h)])
                for h in H2:
                    ACT(hs(MU, h), hs(MU, h), AF.Identity, [MU.t(h), tV], [MU.t(h)], scale=vcol(i_gnw(l, c)), bias=vcol(i_gnb(l, c)))
                for h in H2:
                    TT(hs(MU, h), hs(MU, h), hs(BON, h), ALU.add, [MU.t(h), BON.t(h)], [MU.t(h)])
                for h in H2:
                    TT(OR_.a[:, c, HS[h]], hs(MU, h), hs(GATE, h), ALU.mult, [MU.t(h), GATE.t(h)], [OR_.t(c)])


            pp = ExitStack()
            bufs = prepA(0, pp)
            if prepB(0, *bufs):
                return True
            S.barrier()
            pp.close()
            for c in range(4):
                if dbg and stop == "bar":
                    dump("gate", GATE.a[:], [GATE.t(0), GATE.t(1)], [128, NTOK], BF16)
                    return True
                if groups_(c):
                    return True
                if dbg and stop == f"y{c}":
                    dump("yacc", YACC.a[:], [YACC.t(0), YACC.t(1)], [128, NTOK])
                    return True
                pp = ExitStack()
                if c < 3:
                    bufs = prepA(c + 1, pp)
                oo = ExitStack()
                output_(c, oo)
                if c < 3:
                    if prepB(c + 1, *bufs):
                        return True
                S.barrier()
                oo.close()
                pp.close()

            if dbg and stop == "rwkv":
                dump("or", OR_.a[:].rearrange("p c t -> p (c t)"), list(OR_.ts.values()), [128, 4 * NTOK], BF16)
                return True
            return False

        def attention(u, l, B, TL, OA_, at):
            NK = TL + (256 if u == 1 else 0)
            KOFF = NK - TL
            nkt = NK // 128
            QT_ = mk("QTb", [128, 4, NTOK], BF16, at)
            K32 = mk("K32", [128, 1280], F32, at)
            KT2 = mk("KT2", [128, 2, 1280], BF16, at)
            VTM = mk("VTM", [128, 10, 128], BF16, at)
            Q32 = [mk(f"Q32{i}", [128, 512], F32, at) for i in range(2)]
            SQ = [mk(f"aSQ{i}", [128, 512], BF16, at) for i in range(2)]
            R1 = [mk(f"aR{i}", [128, 512], F32, at) for i in range(2)]
            PT = [mk(f"PT{i}", [128, 512], BF16, at) for i in range(4)]
            RCPS = [mk(f"RCP{i}", [128, 512], F32, at) for i in range(2)]
            RSS = [mk(f"aRS{i}", [128, 512], F32, at) for i in range(2)]
            if u == 1:
                ROPE = mk("ROPE", [128, 2048], F32, at)
                CK = mk("CKl", [128, 2, 128], F32, at)
                S.dma("sp", ROPE.a[:], rope_d, [], [ROPE.t()])
                S.dma("sp", CK.a[:], ck_d[l].rearrange("(a p) f -> p a f", p=128), [], [CK.t()])
                S.dma("pool", VTM.a[:, 0:2, :], cv_d[l].rearrange("(a p) f -> p a f", p=128), [], [VTM.t()])
                pb, tpb = ps_rot()
                for a in range(2):
                    MM(pb[:, a * 128:(a + 1) * 128], CK.a[:, a, :], IDENT, [CK.t(), tCF], [tpb])
                CP(K32.a[:, 0:256], pb[:, 0:256], [tpb], [K32.t()])
            wq, twq, wiq = ws.pop(f"q{u}{l}")
            wqv = wq[:, 0:4096].rearrange("p (a b) -> p a b", b=512)
            wk, twk, wik = ws.pop(f"kv{u}{l}")
            wkv = wk[:, 0:2048].rearrange("p (a b) -> p a b", b=256)

            def qk_norm(h, ch):
                pb, tpb = ps_rot()
                for kc in range(8):
                    lhs = wqv[:, kc, ch * 128:(ch + 1) * 128] if ch < 4 else wkv[:, kc, 0:128]
                    MM(pb[:], lhs, H.a[:, kc, HS[h]], [twq if ch < 4 else twk, H.t(kc, h)], [tpb], start=(kc == 0), stop=(kc == 7))
                q32 = Q32[ch % 2]
                RS = RSS[ch % 2]
                CP(q32.a[:], pb[:], [tpb], [q32.t()], eng="act")
                sq = SQ[ch % 2]
                ACT(sq.a[:], q32.a[:], AF.Square, [q32.t()], [sq.t()])
                pb2, tpb2 = ps_rot()
                MM(pb2[:], BLK16, sq.a[:], [tCB, sq.t()], [tpb2])
                RSQ(RS.a[:], pb2[:], 1.0 / 64, NORM_EPS, [tpb2], [RS.t()])
                gcol = vcol(i_qg(l)) if ch < 4 else vcol(i_kg(l))
                if ch < 4:
                    dst, tdst = QT_.a[:, ch, HS[h]], QT_.t(ch, h)
                else:
                    dst, tdst = None, None
                if u == 0:
                    if ch < 4:
                        STT(dst, q32.a[:], gcol, RS.a[:], ALU.mult, ALU.mult, [q32.t(), tV, RS.t()], [tdst])
                    else:
                        STT(K32.a[:, h * 512:(h + 1) * 512], q32.a[:], gcol, RS.a[:], ALU.mult, ALU.mult, [q32.t(), tV, RS.t()], [K32.t()])
                else:
                    STT(q32.a[:], q32.a[:], gcol, RS.a[:], ALU.mult, ALU.mult, [q32.t(), tV, RS.t()], [q32.t()])
                    pb3, tpb3 = ps_rot()
                    MM(pb3[:], ROT, q32.a[:], [tCF, q32.t()], [tpb3])
                    r1 = R1[0]
                    r2 = R1[1]
                    TT(r1.a[:], q32.a[:], ROPE.a[:, h * 512:(h + 1) * 512], ALU.mult, [q32.t(), ROPE.t()], [r1.t()])
                    TT(r2.a[:], pb3[:], ROPE.a[:, 1024 + h * 512:1024 + (h + 1) * 512], ALU.mult, [tpb3, ROPE.t()], [r2.t()])
                    if ch < 4:
                        TT(dst, r1.a[:], r2.a[:], ALU.add, [r1.t(), r2.t()], [tdst])
                    else:
                        TT(K32.a[:, 256 + h * 512:256 + (h + 1) * 512], r1.a[:], r2.a[:], ALU.add, [r1.t(), r2.t()], [K32.t()])

            for h in range(2):
                for ch in range(5):
                    qk_norm(h, ch)
            for tt in range(8):
                pb, tpb = ps_rot()
                for kc in range(8):
                    MM(pb[:, 0:128], H.a[:, kc, tt * 128:(tt + 1) * 128], wkv[:, kc, 128:256], [H.t(kc, tt // 4), twk], [tpb], start=(kc == 0), stop=(kc == 7))
                if u == 0:
                    CP(VTM.a[:, tt, :], pb[:, 0:128], [tpb], [VTM.t()], eng="act")
                    stg = STG[tt % 2]
                    CP(stg.a[:, 0:128], pb[:, 0:128], [tpb], [stg.t()])
                    out_deps.append(S.dma("sp", ncv_d[tt // 2, l, (tt % 2) * 128:(tt % 2 + 1) * 128, :], stg.a[:, 0:128], [stg.t()], []))
                else:
                    CP(VTM.a[:, 2 + tt, :], pb[:, 0:128], [tpb], [VTM.t()], eng="act")
            ws.rel(wiq)
            ws.rel(wik)
            if u == 0:
                for tt in range(8):
                    pb, tpb = ps_rot()
                    MM(pb[:, 0:128], K32.a[:, tt * 128:(tt + 1) * 128], IDENT, [K32.t(), tCF], [tpb])
                    stg = STG[tt % 2]
                    CP(stg.a[:, 0:128], pb[:, 0:128], [tpb], [stg.t()])
                    out_deps.append(S.dma("sp", nck_d[tt // 2, l, (tt % 2) * 128:(tt % 2 + 1) * 128, :], stg.a[:, 0:128], [stg.t()], []))
            ncol = 1024 if u == 0 else 1280
            for kvh in range(2):
                for c0 in range(0, ncol, 512):
                    w = min(512, ncol - c0)
                    pb, tpb = ps_rot()
                    MM(pb[:, 0:w], SEL[kvh], K32.a[:, c0:c0 + w], [tCF, K32.t()], [tpb])
                    EV(KT2.a[:, kvh, c0:c0 + w], pb[:, 0:w], [tpb], [KT2.t()])
            if dbg and stop == "qk":
                dump("qt", QT_.a[:].rearrange("p c t -> p (c t)"), list(QT_.ts.values()), [128, 4 * NTOK], BF16)
                dump("kt2", KT2.a[:].rearrange("p c t -> p (c t)"), KT2.t(), [128, 2 * 1280], BF16)
                dump("vtm", VTM.a[:].rearrange("p c t -> p (c t)"), VTM.t(), [128, 1280], BF16)
                return True
            QB = 512 if u == 1 else 256
            items = []
            seti = 0
            for b in range(B):
                for qb in range(TL // QB):
                    q0 = b * TL + qb * QB
                    for qc in range(4):
                        bank = 4 + 2 * (seti % 2)
                        rcp = RCPS[seti % 2]
                        seti += 1
                        for hh in range(2):
                            for kt in range(nkt):
                                items.append(dict(b=b, q0=q0, qc=qc, hh=hh, kt=kt, bank=bank, rcp=rcp,
                                                  last=(hh == 1 and kt == nkt - 1)))
            pti = [0]

            def emit_score(it):
                hh, qc, kt, b, q0 = it["hh"], it["qc"], it["kt"], it["b"], it["q0"]
                kvh = (qc * 2 + hh) // 4
                pr = slice(hh * 64, (hh + 1) * 64)
                kcol = (b * TL if u == 0 else 0) + kt * 128
                psx, tpsx = ps_rot(0, 4)
                MM(psx[:, 0:QB], KT2.a[pr, kvh, kcol:kcol + 128], QT_.a[pr, qc, q0:q0 + QB], [KT2.t(), QT_.t(qc, q0 // 512)], [tpsx])
                pt = PT[pti[0] % 4]
                pti[0] += 1
                ACT(pt.a[:, 0:QB], psx[:, 0:QB], AF.Exp, [tpsx], [pt.t()], scale=0.125)
                it["pt"] = pt

            def emit_pv(it):
                hh, qc, kt, b, q0 = it["hh"], it["qc"], it["kt"], it["b"], it["q0"]
                kvh = (qc * 2 + hh) // 4
                pr = slice(hh * 64, (hh + 1) * 64)
                vt_i = (b * 2 + kt) if u == 0 else kt
                po, tpo = PSB[it["bank"]]
                pr_, tpr = PSB[it["bank"] + 1]
                pt = it["pt"]
                MM(po[pr, 0:QB], VTM.a[:, vt_i, kvh * 64:(kvh + 1) * 64], pt.a[:, 0:QB], [VTM.t(), pt.t()], [tpo], start=(kt == 0), stop=(kt == nkt - 1))
                MM(pr_[pr, 0:QB], ONESB, pt.a[:, 0:QB], [tCB, pt.t()], [tpr], start=(kt == 0), stop=(kt == nkt - 1))
                if it["last"]:
                    rcp = it["rcp"]
                    S.op("act", [tpr], [rcp.t()], lambda e: e.activation(out=rcp.a[:, 0:QB], in_=pr_[:, 0:QB], func=AF.Ln))
                    S.op("act", [rcp.t()], [rcp.t()], lambda e: e.activation(out=rcp.a[:, 0:QB], in_=rcp.a[:, 0:QB], func=AF.Exp, scale=-1.0))
                    TT(OA_.a[:, qc, q0:q0 + QB], po[:, 0:QB], rcp.a[:, 0:QB], ALU.mult, [tpo, rcp.t()], [OA_.t(qc)])

            LOOK = 3
            for i in range(min(LOOK, len(items))):
                emit_score(items[i])
            for i in range(len(items)):
                if i + LOOK < len(items):
                    emit_score(items[i + LOOK])
                emit_pv(items[i])
            if dbg and stop == "attn":
                dump("oa", OA_.a[:].rearrange("p c t -> p (c t)"), list(OA_.ts.values()), [128, 4 * NTOK], BF16)
                return True
            return False

        def merge(u, l, OR_, OA_, mg):
            MG = mk("MG", [128, 8, NTOK], BF16, mg)
            M32 = mk("M32", [128, 8, 512], F32, mg)
            WBR = mk("WBR", [128, 8, 1024], BF16, mg)
            SG = [mk(f"SG{i}", [128, 512], F32, mg) for i in range(2)]
            TQ = [mk(f"TQ{i}", [128, 512], F32, mg) for i in range(2)]
            SQ = [mk(f"mSQ{i}", [128, 512], BF16, mg) for i in range(2)]
            RS = mk("mRS", [128, 512], F32, mg)
            TMP = [mk(f"mTMP{i}", [128, 512], F32, mg) for i in range(2)]
            S.dma("pool", WBR.a[:], w_br[l], [], [WBR.t()])
            for jj in range(2):
                gr, tgr, wir = ws.pop(f"g{u}{l}{jj}")
                ga, tga, wia = ws.pop(f"g{u}{l}{2 + jj}")
                gw = [gr[:, 0:4096].rearrange("p (a b) -> p a b", b=512), ga[:, 0:4096].rearrange("p (a b) -> p a b", b=512)]
                tg = [tgr, tga]
                for j4 in range(4):
                    j = jj * 4 + j4
                    for h in range(2):
                        for i in range(2):
                            pg, tpg = ps_rot()
                            for kc in range(8):
                                MM(pg[:], gw[i][:, kc, j4 * 128:(j4 + 1) * 128], H.a[:, kc, HS[h]], [tg[i], H.t(kc, h)], [tpg], start=(kc == 0), stop=(kc == 7))
                            ACT(SG[i].a[:], pg[:], AF.Sigmoid, [tpg], [SG[i].t()])
                            pbr, tpbr = ps_rot()
                            SRC = OR_ if i == 0 else OA_
                            for kc in range(4):
                                MM(pbr[:], WBR.a[:, i * 4 + kc, j * 128:(j + 1) * 128], SRC.a[:, kc, HS[h]], [WBR.t(), SRC.t(kc)], [tpbr], start=(kc == 0), stop=(kc == 3))
                            TT(TQ[i].a[:], pbr[:], SG[i].a[:], ALU.mult, [tpbr, SG[i].t()], [TQ[i].t()])
                        TT(MG.a[:, j, HS[h]], TQ[0].a[:], TQ[1].a[:], ALU.add, [TQ[0].t(), TQ[1].t()], [MG.t(j, h)])
                ws.rel(wir)
                ws.rel(wia)
            if dbg and stop == "mg":
                dump("mg", MG.a[:].rearrange("p c t -> p (c t)"), list(MG.ts.values()), [128, 8 * NTOK], BF16)
                return True
            wo = [ws.pop(f"wo{u}{l}{jt}") for jt in range(2)]
            for h in range(2):
                for j in range(8):
                    wsl, twsl, _ = wo[j // 4]
                    wv = wsl[:, 0:4096].rearrange("p (a b) -> p a b", b=512)
                    pb, tpb = ps_rot()
                    for kc in range(8):
                        MM(pb[:], wv[:, kc, (j % 4) * 128:(j % 4 + 1) * 128], MG.a[:, kc, HS[h]], [twsl, MG.t(kc, h)], [tpb], start=(kc == 0), stop=(kc == 7))
                    EV(M32.a[:, j, :], pb[:], [tpb], [M32.t(j)])
                resid_add(u, l, M32, D_G1, h, (SQ, RS, TMP))
            ws.rel(wo[0][2])
            ws.rel(wo[1][2])
            return False

        def mlp(u, l, ml):
            UB = mk("UB", [128, 32, NTOK], BF16, ml)
            FB = mk("FB", [128, 8, 512], F32, ml)
            RL = [mk("RL0", [128, 512], F32, ml)] * 2
            SQ = [mk("fSQ0", [128, 512], BF16, ml)] * 2
            RS = mk("fRS", [128, 512], F32, ml)
            TMP = [mk("fTMP0", [128, 512], F32, ml)] * 2
            k = 0
            for tb in range(8):
                wsl, twsl, wi = ws.pop(f"up{u}{l}{tb}")
                wv = wsl[:, 0:4096].rearrange("p (a b) -> p a b", b=512)
                for h in range(2):
                    for f4 in range(4):
                        fc = tb * 4 + f4
                        pb, tpb = ps_rot()
                        for kc in range(8):
                            MM(pb[:], wv[:, kc, f4 * 128:(f4 + 1) * 128], H.a[:, kc, HS[h]], [twsl, H.t(kc, h)], [tpb], start=(kc == 0), stop=(kc == 7))
                        rl = RL[k % 2]
                        k += 1
                        ACT(rl.a[:], pb[:], AF.Relu, [tpb], [rl.t()])
                        TT(UB.a[:, fc, HS[h]], rl.a[:], rl.a[:], ALU.mult, [rl.t()], [UB.t(fc, h)])
                ws.rel(wi)
            FB1 = mk("FB1", [128, 8, 512], F32, ml)
            FBS = [FB, FB1]
            for j in range(8):
                wsl, twsl, wi = ws.pop(f"dn{u}{l}{j}")
                wv = wsl[:, 0:4096].rearrange("p (a b) -> p a b", b=128)
                for h in range(2):
                    pb, tpb = ps_rot()
                    for fc in range(32):
                        MM(pb[:], wv[:, fc, :], UB.a[:, fc, HS[h]], [twsl, UB.t(fc, h)], [tpb], start=(fc == 0), stop=(fc == 31))
                    EV(FBS[h].a[:, j, :], pb[:], [tpb], [FBS[h].t(j)])
                ws.rel(wi)
            for h in range(2):
                resid_add(u, l, FBS[h], D_G2, h, (SQ, RS, TMP))
            return False

        done = False
        for u in units:
            load_x(u)
            S.barrier()
            for l in layers:
                if layer(u, l):
                    done = True
                    break
            if done:
                break
            store_y(u)
        S.finish(out_deps)
        build.stats = dict(cnt=dict(S.cnt), nsem=S.nsem, nwait=S.nwait)
    return nc, dumps


def _consts():
    p = np.arange(128)
    ident = np.eye(128, dtype=np.float32)
    blk = (p[:, None] // 64 == p[None, :] // 64).astype(np.float32)
    ones = np.ones((128, 128), np.float32)
    rot = np.zeros((128, 128), np.float32)
    for m in range(128):
        d = m % 64
        half = (d % 32) // 16
        if half == 0:
            rot[m + 16, m] = -1.0
        else:
            rot[m - 16, m] = 1.0
    sel = []
    for kvh in range(2):
        s = np.zeros((128, 128), np.float32)
        for m in range(128):
            s[kvh * 64 + m % 64, m] = 1.0
        sel.append(s)
    cstf = np.concatenate([ident, blk, ones, rot, sel[0], sel[1]], axis=1)
    j = p[None, :]
    pp = p[:, None]
    sl = (j < pp).astype(np.float32)
    su = (j > pp).astype(np.float32)
    il = (j <= pp).astype(np.float32)
    iu = (j >= pp).astype(np.float32)
    maska = np.concatenate([sl, sl, su, su], 1)
    maskb = np.concatenate([su, su, sl, sl], 1)
    maskc = np.concatenate([iu, iu, il, il], 1)
    lms = []
    for lv in range(7):
        s_ = 1 << lv
        m = ((pp // (2 * s_) == j // (2 * s_)) & (pp % (2 * s_) < s_) & (j % (2 * s_) >= s_)).astype(np.float32)
        lms.append(np.concatenate([m, m, m.T, m.T], 1))
    cstb = np.concatenate([maska, maskb, maskc, ident, np.ones((128, 64), np.float32)] + lms + [ones, blk], axis=1)
    t = np.arange(1024)
    d = p % 64
    axis = d // 32
    f = d % 16
    freqs = (1.0 / (10000.0 ** (np.arange(0, 32, 2, dtype=np.float32) / 32.0))).astype(np.float32)
    pos = np.where(axis[:, None] == 0, (t // 64)[None, :], (t % 64)[None, :]).astype(np.float32)
    ang = pos * freqs[f][:, None]
    rope = np.concatenate([np.cos(ang), np.sin(ang)], axis=1).astype(np.float32)
    return np.ascontiguousarray(cstf), np.ascontiguousarray(cstb), np.ascontiguousarray(rope)


def _vtable(inp, core):
    rows = np.zeros((NV, 128), np.float32)
    r = lambda a: np.asarray(a, np.float32).reshape(-1, 128)
    rows[0:64] = r(inp["norm_g"])
    rows[64:160] = r(inp["b_mod"])
    rows[160:184] = r(inp["rwkv_mu"])
    rows[184:192] = r(inp["rwkv_k_k"])
    rows[192:200] = r(inp["rwkv_k_a"])
    rows[200:208] = r(inp["rwkv_r_k"])
    rows[208:224] = r(inp["decay_w0"])
    rows[224:240] = r(inp["iclr_a0"])
    rows[240:248] = r(inp["gn_w"])
    rows[248:256] = r(inp["gn_b"])
    rows[256:258] = np.tile(np.asarray(inp["q_gain"], np.float32), (1, 2))
    rows[258:260] = np.tile(np.asarray(inp["k_gain"], np.float32), (1, 2))
    rows[260:268] = r(inp["c_ctx"])
    rows[268:276] = r(np.asarray(inp["c"])[core // 2])
    return rows


_CACHE = {}


def _get_nc(**kw):
    key = tuple(sorted((k, str(v)) for k, v in kw.items()))
    if key not in _CACHE:
        _CACHE[key] = build(**kw)
    return _CACHE[key]


def make_in_maps(inp):
    cstf, cstb, rope = _consts()
    f = lambda a: np.ascontiguousarray(np.asarray(a, np.float32))
    shared = {k: f(inp[k]) for k in ("w_in", "w_br", "w_out", "w_mod", "mlp_up", "mlp_down", "decay_up", "iclr_up", "gate_up")}
    xp = f(inp["x_prompt"])
    xs = f(inp["x_sample"])
    ck = f(inp["cache_k"])
    cv = f(inp["cache_v"])
    st = f(inp["state_wkv"])
    maps = []
    for i in range(8):
        b = i // 2
        m = dict(shared)
        m["xp"] = np.ascontiguousarray(xp[4 * i:4 * i + 4].reshape(NTOK, DM))
        m["xs"] = np.ascontiguousarray(xs[b])
        m["ck"] = np.ascontiguousarray(ck[b].reshape(NL, 256, 128))
        m["cv"] = np.ascontiguousarray(cv[b].reshape(NL, 256, 128))
        m["st"] = np.ascontiguousarray(st[b])
        m["vt"] = _vtable(inp, i)
        m["cstf"] = cstf
        m["cstb"] = cstb
        m["rope"] = rope
        maps.append(m)
    return maps


def kernel(**inputs):
    nc, _ = _get_nc()
    maps = make_in_maps(inputs)
    res = run_bass_kernel_spmd(nc, maps, core_ids=list(range(8)))
    R = res.results
    y_prompt = np.concatenate([R[i]["yp"].reshape(4, 256, DM) for i in range(8)], axis=0).astype(np.float32)
    y_sample = np.stack([R[2 * b]["ys"] for b in range(4)], axis=0).astype(np.float32)
    nck = np.concatenate([R[i]["nck"].reshape(4, NL, 256, 2, 64) for i in range(8)], axis=0).astype(np.float32)
    ncv = np.concatenate([R[i]["ncv"].reshape(4, NL, 256, 2, 64) for i in range(8)], axis=0).astype(np.float32)
    nst = np.concatenate([R[i]["nst"] for i in range(8)], axis=0).astype(np.float32)
    return (y_prompt, y_sample, nck, ncv, nst)
```
